# Optimizing a Trainium2 kernel written in Bass

```python
import jax, jax.numpy as jnp
from jax import lax
import numpy as np

D_MODEL = 2048
BATCH = 4
SEQ = 4096
DEPTH = 1

D_MIX = D_MODEL
D_A = D_MIX // 2
CHUNK = 128
A_GROUPS = 8
A_GROUP_W = D_A // A_GROUPS
D_B = D_MIX - D_A
HEAD_DIM = 64
N_Q_HEADS = D_B // HEAD_DIM
N_KV_HEADS = 4
Q_PER_KV = N_Q_HEADS // N_KV_HEADS
D_KV = N_KV_HEADS * HEAD_DIM
WINDOW = 128
BLOCK = WINDOW
ROPE_THETA = 10000.0
NORM_EPS = 1e-5
SPLIT_SIZES = (D_A, D_A, D_A, D_B, D_KV, D_KV, D_B)
D_IN = sum(SPLIT_SIZES)

kernel_name = "hymba_gmlp_swa_sink_adaln"


def rms_norm(x, g):
    xf = x.astype(jnp.float32)
    y = xf * lax.rsqrt(jnp.mean(xf * xf, axis=-1, keepdims=True) + NORM_EPS)
    return (y * g.astype(jnp.float32)).astype(x.dtype)


def layer_norm(x, g, b):
    xf = x.astype(jnp.float32)
    mu = jnp.mean(xf, axis=-1, keepdims=True)
    xc = xf - mu
    var = jnp.mean(xc * xc, axis=-1, keepdims=True)
    y = xc * lax.rsqrt(var + NORM_EPS)
    return (y * g.astype(jnp.float32) + b.astype(jnp.float32)).astype(x.dtype)


def modulate(h, shift, scale):
    return h * (1.0 + scale[:, None, :]) + shift[:, None, :]


def rope_tables(seq, dtype):
    inv_freq = ROPE_THETA ** (-jnp.arange(0, HEAD_DIM, 2, dtype=jnp.float32) / HEAD_DIM)
    ang = jnp.arange(seq, dtype=jnp.float32)[:, None] * inv_freq[None, :]
    return jnp.cos(ang).astype(dtype), jnp.sin(ang).astype(dtype)


def apply_rope(x, cos, sin):
    x1, x2 = jnp.split(x, 2, axis=-1)
    c = cos[None, :, None, :]
    s = sin[None, :, None, :]
    return jnp.concatenate([x1 * c - x2 * s, x2 * c + x1 * s], axis=-1)


def chunked_spatial_gating(u, v, ln_g, ln_b, w_s, b_s):
    bsz, seq, _ = v.shape
    n_chunks = seq // CHUNK
    v = layer_norm(v, ln_g, ln_b)
    vg = v.reshape(bsz, n_chunks, CHUNK, A_GROUPS, A_GROUP_W)
    w = w_s * jnp.tril(jnp.ones((CHUNK, CHUNK), w_s.dtype))[None]
    s = jnp.einsum('gts,bcsgd->bctgd', w, vg) + b_s.T[None, None, :, :, None]
    return u * s.reshape(bsz, seq, D_A)


def sliding_window_sink_attention(q, k, v, sinks):
    bsz, seq = q.shape[0], q.shape[1]
    nb = seq // BLOCK
    qb = q.reshape(bsz, nb, BLOCK, N_KV_HEADS, Q_PER_KV, HEAD_DIM)
    kb = k.reshape(bsz, nb, BLOCK, N_KV_HEADS, HEAD_DIM)
    vb = v.reshape(bsz, nb, BLOCK, N_KV_HEADS, HEAD_DIM)
    pad = ((0, 0), (1, 0), (0, 0), (0, 0), (0, 0))
    k_band = jnp.concatenate([jnp.pad(kb, pad)[:, :-1], kb], axis=2)
    v_band = jnp.concatenate([jnp.pad(vb, pad)[:, :-1], vb], axis=2)
    scale = HEAD_DIM ** -0.5
    scores = jnp.einsum('bnqkgd,bnjkd->bnkgqj', qb, k_band).astype(jnp.float32) * scale
    blk = jnp.arange(nb)[:, None]
    qpos = blk * BLOCK + jnp.arange(BLOCK)[None, :]
    kpos = (blk - 1) * BLOCK + jnp.arange(2 * BLOCK)[None, :]
    rel = qpos[:, :, None] - kpos[:, None, :]
    valid = (rel >= 0) & (rel < WINDOW) & (kpos[:, None, :] >= 0)
    scores = jnp.where(valid[None, :, None, None, :, :], scores, -jnp.inf)
    sink = sinks.astype(jnp.float32).reshape(N_KV_HEADS, Q_PER_KV)[None, None, :, :, None, None]
    m = jnp.maximum(jnp.max(scores, axis=-1, keepdims=True), sink)
    p = jnp.exp(scores - m)
    denom = jnp.sum(p, axis=-1, keepdims=True) + jnp.exp(sink - m)
    probs = (p / denom).astype(v.dtype)
    out = jnp.einsum('bnkgqj,bnjkd->bnqkgd', probs, v_band)
    return out.reshape(bsz, seq, N_Q_HEADS * HEAD_DIM)


def setup_inputs(seed: int = 0) -> dict:
    key = jax.random.key(seed)
    ks = jax.random.split(key, 16)
    f32 = jnp.float32
    ada_std = 0.2 * D_MODEL ** -0.5
    return {
        "x": jax.random.normal(ks[0], (BATCH, SEQ, D_MODEL), f32),
        "c": jax.random.normal(ks[1], (BATCH, D_MODEL), f32),
        "w_ada": jax.random.normal(ks[2], (DEPTH, D_MODEL, 3 * D_MODEL), f32) * ada_std,
        "b_ada": jax.random.normal(ks[3], (DEPTH, 3 * D_MODEL), f32) * 0.02,
        "norm_g": 1.0 + 0.02 * jax.random.normal(ks[4], (DEPTH, D_MODEL), f32),
        "w_in": jax.random.normal(ks[5], (DEPTH, D_MODEL, D_IN), f32) * D_MODEL ** -0.5,
        "ln_v_g": 1.0 + 0.02 * jax.random.normal(ks[6], (DEPTH, D_A), f32),
        "ln_v_b": 0.02 * jax.random.normal(ks[7], (DEPTH, D_A), f32),
        "w_spatial": jax.random.normal(ks[8], (DEPTH, A_GROUPS, CHUNK, CHUNK), f32) * (0.5 * CHUNK ** -0.5),
        "b_spatial": 1.0 + 0.02 * jax.random.normal(ks[9], (DEPTH, A_GROUPS, CHUNK), f32),
        "sinks": 0.5 * jax.random.normal(ks[10], (DEPTH, N_Q_HEADS), f32),
        "w_out": jax.random.normal(ks[11], (DEPTH, D_MIX, D_MODEL), f32) * D_MIX ** -0.5,
        "w_ada_final": jax.random.normal(ks[12], (D_MODEL, 2 * D_MODEL), f32) * ada_std,
        "b_ada_final": jax.random.normal(ks[13], (2 * D_MODEL,), f32) * 0.02,
        "final_norm_g": 1.0 + 0.02 * jax.random.normal(ks[14], (D_MODEL,), f32),
    }


def reference(x, c, w_ada, b_ada, norm_g, w_in, ln_v_g, ln_v_b, w_spatial, b_spatial,
              sinks, w_out, w_ada_final, b_ada_final, final_norm_g):
    bsz, seq, _ = x.shape
    c_act = jax.nn.silu(c)
    cos, sin = rope_tables(seq, x.dtype)
    offs = np.cumsum((0,) + SPLIT_SIZES)[:-1].tolist()[1:]
    for l in range(DEPTH):
        mod = c_act @ w_ada[l] + b_ada[l]
        shift, scale, gate = jnp.split(mod, 3, axis=-1)
        h = modulate(rms_norm(x, norm_g[l]), shift, scale)
        proj = h @ w_in[l]
        u_a, v_a, z_a, q, k, v, z_b = jnp.split(proj, offs, axis=-1)
        y_a = chunked_spatial_gating(u_a, v_a, ln_v_g[l], ln_v_b[l], w_spatial[l], b_spatial[l])
        q = apply_rope(q.reshape(bsz, seq, N_Q_HEADS, HEAD_DIM), cos, sin)
        k = apply_rope(k.reshape(bsz, seq, N_KV_HEADS, HEAD_DIM), cos, sin)
        v = v.reshape(bsz, seq, N_KV_HEADS, HEAD_DIM)
        y_b = sliding_window_sink_attention(q, k, v, sinks[l])
        y = jnp.concatenate([y_a * jax.nn.silu(z_a), y_b * jax.nn.silu(z_b)], axis=-1)
        x = x + gate[:, None, :] * (y @ w_out[l])
    mod_f = c_act @ w_ada_final + b_ada_final
    shift_f, scale_f = jnp.split(mod_f, 2, axis=-1)
    return modulate(rms_norm(x, final_norm_g), shift_f, scale_f)
```

```python
import os
import numpy as np
import ml_dtypes
from contextlib import ExitStack
import concourse.bass as bass
import concourse.mybir as mybir
from concourse.bass_utils import run_bass_kernel_spmd

F32 = mybir.dt.float32
BF16 = mybir.dt.bfloat16
AF = mybir.ActivationFunctionType
ALU = mybir.AluOpType

NCORES = 8
D = 2048
TOK = 2048
NB = 16
EPS = 1e-5
NCH_IN = 22
KDMA = 8


class Res:
    __slots__ = ("name", "writers", "readers", "excl")

    def __init__(self, name, excl=False):
        self.name = name
        self.writers = []
        self.readers = []
        self.excl = excl


class Op:
    __slots__ = ("eng", "fn", "deps", "sig", "dma", "dma_idx", "cnt")


class Sched:
    ENGS = ("pe", "act", "dve", "pool", "sp")

    def __init__(self):
        self.ops = {e: [] for e in self.ENGS}
        self.ndma = {e: 0 for e in self.ENGS}
        self.dma_ops = {e: [] for e in self.ENGS}
        self.floor = {}
        self.muted = False
        self.nadd = 0
        self.maxops = int(os.environ.get("KMAXOPS", "100000000"))

    def add(self, eng, fn, reads=(), writes=(), accum=(), dma=False):
        if self.muted:
            return None
        self.nadd += 1
        if getattr(self, "trace_lbl", None) is not None:
            self.trace_lbl.append((self.nadd, eng, getattr(self, "cur_lbl", "?")))
        if self.nadd > self.maxops and not getattr(self, "force", False):
            return None
        lst = self.ops[eng]
        idx = len(lst)
        deps = dict(self.floor.get(eng, {}))
        if eng in self.floor:
            del self.floor[eng]

        def need(h):
            e, i = h
            if e == eng and i >= idx:
                return
            if e == "pe" and eng == "pe":
                return
            key = (e, i) if self.ops[e][i].dma else (e, None)
            if key[1] is None:
                deps[key] = max(deps.get(key, -1), i)
            else:
                deps[key] = i

        for r in reads:
            for h in r.writers:
                need(h)
            if r.excl:
                for h in r.readers:
                    if h[0] != eng:
                        need(h)
        for r in writes:
            for h in r.writers:
                need(h)
            for h in r.readers:
                need(h)
        for r in accum:
            for h in r.readers:
                need(h)
            if r.writers:
                need(r.writers[0])
        op = Op()
        op.eng = eng
        op.fn = fn
        op.deps = [(k[0], v) for k, v in deps.items()]
        op.sig = False
        op.dma = dma
        op.dma_idx = None
        op.cnt = None
        if dma:
            op.dma_idx = self.ndma[eng]
            self.ndma[eng] += 1
            self.dma_ops[eng].append(idx)
            j = op.dma_idx
            if j >= KDMA:
                op.deps.append((eng, self.dma_ops[eng][j - KDMA]))
        lst.append(op)
        h = (eng, idx)
        for r in reads:
            r.readers.append(h)
        for r in writes:
            r.writers = [h]
            r.readers = []
        for r in accum:
            r.writers.append(h)
        return h

    def barrier(self):
        fl = {}
        for e in self.ENGS:
            n = len(self.ops[e])
            if n == 0:
                continue
            for i in range(n - 1, -1, -1):
                if not self.ops[e][i].dma:
                    fl[(e, None)] = i
                    break
            for i in self.dma_ops[e][-KDMA:]:
                fl[(e, i)] = i
        for e in self.ENGS:
            self.floor[e] = dict(fl)

    def finalize(self):
        for e in self.ENGS:
            for op in self.ops[e]:
                for (de, di) in op.deps:
                    t = self.ops[de][di]
                    if not t.dma:
                        t.sig = True
        for e in self.ENGS:
            c = 0
            for op in self.ops[e]:
                if (not op.dma) and op.sig:
                    c += 1
                op.cnt = c

    def emit(self, ename, engobj, esem, dsems):
        waited = {}
        for op in self.ops[ename]:
            for (de, di) in op.deps:
                t = self.ops[de][di]
                if t.dma:
                    sem = dsems[de][t.dma_idx % KDMA]
                    val = 16 * (t.dma_idx // KDMA + 1)
                else:
                    sem = esem[de]
                    val = t.cnt
                key = id(sem)
                if waited.get(key, 0) < val:
                    engobj.wait_ge(sem, val)
                    waited[key] = val
            inst = op.fn(engobj)
            if op.dma:
                inst.then_inc(dsems[ename][op.dma_idx % KDMA], 16)
            elif op.sig:
                inst.then_inc(esem[ename], 1)
        return waited


def build_nc():
    nc = bass.Bass("TRN2", target_bir_lowering=False)

    def din(name, shape, dt=F32):
        return nc.dram_tensor(name, list(shape), dt, kind="ExternalInput").ap()

    xs = din("xs", [17 * 128, D])
    c_col = din("c_col", [128, 16])
    w_ada = din("w_ada", [D, 6144])
    w_adaf = din("w_adaf", [D, 4096])
    b_col = din("b_col", [128, 80])
    g_col = din("g_col", [128, 16])
    gf_col = din("gf_col", [128, 16])
    w_in = din("w_in", [D, 5632])
    w_out = din("w_out", [D, D])
    lg_col = din("lg_col", [128, 8])
    lb_row = din("lb_row", [1, 1024])
    wst_in = din("wst", [128, 8 * 128])
    bs_row = din("bs_row", [1, 1024])
    tri_in = din("tri", [128, 128])
    sinks_b = din("sinks_b", [128, 16])
    ident_in = din("ident", [128, 128], BF16)
    identf_in = din("identf", [128, 128])
    pswap_in = din("pswap", [128, 128], BF16)
    masks_in = din("masks", [128, 2 * 512], BF16)
    cos_in = din("cosT", [128, 17 * 128])
    sin_in = din("sinT", [128, 17 * 128])
    out = nc.dram_tensor("out", [TOK, D], F32, kind="ExternalOutput").ap()
    wo_bf = nc.dram_tensor("wo_bf", [D, D], BF16, kind="Internal").ap()
    DBG = os.environ.get("KDBG") == "1"
    STOP = int(os.environ.get("KSTOP", "99"))
    if DBG:
        dbg_mod = nc.dram_tensor("dbg_mod", [128, 80], F32, kind="ExternalOutput").ap()
        dbg_yt = nc.dram_tensor("dbg_yt", [128, 16 * 2048], BF16, kind="ExternalOutput").ap()
        dbg_ht = nc.dram_tensor("dbg_ht", [128, 16 * 2048], BF16, kind="ExternalOutput").ap()

    S = Sched()
    if os.environ.get("KTRACE"):
        S.trace_lbl = []
    es = ExitStack()
    with es:
        es.enter_context(nc.allow_low_precision("bf16 matmul operands, fp32 accumulation"))

        def sb(name, shape, dt):
            return es.enter_context(nc.sbuf_tensor(name, list(shape), dt))

        HT = sb("HT", [128, 16 * 2048], BF16)
        HTH = sb("HTH", [128, 16 * 128], BF16)
        YT = sb("YT", [128, 16 * 2048], BF16)
        WB = sb("WB", [128, 4 * 16 * 256], BF16)
        R = sb("R", [128, 9984], F32)
        onesf = sb("onesf", [128, 128], F32)
        rowt = sb("rowt", [128, 512], F32)
        ccol = sb("ccol", [128, 16], F32)
        cact = sb("cact", [128, 16], BF16)
        modcol = sb("modcol", [128, 80], F32)
        bcol = sb("bcol", [128, 80], F32)
        gcol = sb("gcol", [128, 16], F32)
        gfcol = sb("gfcol", [128, 16], F32)
        acol = sb("acol", [128, 16], F32)
        afcol = sb("afcol", [128, 16], F32)
        lgcol = sb("lgcol", [128, 8], F32)
        esk = sb("esk", [128, 16], F32)
        sm = sb("sm", [128, 96], F32)
        PS = es.enter_context(nc.psum_tensor("PS", [128, 4096], F32))

        HT3 = HT[:, :].rearrange("p (k t) -> p k t", k=16)
        HTH3 = HTH[:, :].rearrange("p (k t) -> p k t", k=16)
        YT3 = YT[:, :].rearrange("p (k t) -> p k t", k=16)
        WB4 = WB[:, :].rearrange("p (s k n) -> p s k n", s=4, k=16)

        def bank(b):
            return PS[:, b * 512:(b + 1) * 512]

        def bankbf(b):
            return PS[:, b * 512:(b + 1) * 512].bitcast(BF16)

        def carve(region, off, words, dt=F32):
            ap = region[:, off:off + words]
            if dt is BF16:
                ap = ap.bitcast(BF16)
            return ap

        YTLOW = YT[:, 0:8 * 2048].bitcast(F32)
        HTHW = HTH[:, :].bitcast(F32)
        WBW = WB[:, :].bitcast(F32)

        rBANK = [Res(f"bank{i}", excl=True) for i in range(8)]
        rWB = [Res(f"wb{i}") for i in range(4)]
        rHT = [Res(f"ht{k}") for k in range(16)]
        rHTH = Res("hth")
        rYT = [Res(f"yt{k}") for k in range(16)]
        rWOBF = Res("wobf")

        esem = {e: es.enter_context(nc.semaphore(f"s_{e}")) for e in ("pe", "act", "dve", "pool")}
        dsems = {e: [es.enter_context(nc.semaphore(f"d_{e}{i}")) for i in range(KDMA)] for e in ("sp", "pool")}

        def X(eng, meth, *a, rd=(), wr=(), ac=(), **kw):
            S.cur_lbl = meth + " " + " ".join(f"{k}={getattr(v, 'shape', v)}" for k, v in kw.items() if k in ("out", "in_", "in0", "lhsT", "rhs", "func"))
            return S.add(eng, lambda e: getattr(e, meth)(*a, **kw), reads=rd, writes=wr, accum=ac)

        def dma(q, out_ap, in_ap, rd=(), wr=(), ac=()):
            S.cur_lbl = f"dma out={out_ap.shape} in={in_ap.shape}"
            return S.add(q, lambda e: e.dma_start(out=out_ap, in_=in_ap), reads=rd, writes=wr, accum=ac, dma=True)

        def MM(o, lhsT, rhs, start, stop, rd, bankres, first):
            X("pe", "matmul", o, lhsT=lhsT, rhs=rhs, start=start, stop=stop, rd=rd,
              wr=[bankres] if first else (), ac=() if first else [bankres])

        wslot = [0]

        rSTART = Res("startup_done")
        nload = [0]

        def load_chunk(src, c0):
            s = wslot[0] % 4
            wslot[0] += 1
            src_ap = src.rearrange("(k p) n -> p k n", p=128)[:, :, c0:c0 + 256]
            extra = [rSTART] if 1 <= nload[0] <= 3 else []
            nload[0] += 1
            dma("pool", WB4[:, s, :, :], src_ap, rd=extra, wr=[rWB[s]])
            return s

        ident = carve(R, 8192, 64, BF16)
        rC = Res("consts")
        for (dst, src) in ((ccol[:, :], c_col), (bcol[:, :], b_col), (gcol[:, :], g_col), (gfcol[:, :], gf_col),
                           (lgcol[:, :], lg_col), (esk[:, :], sinks_b), (ident, ident_in)):
            dma("sp", dst, src[:, :], ac=[rC])
        X("dve", "memset", onesf[:, :], 1.0, ac=[rC])
        rMOD = Res("modcol")
        X("dve", "memset", modcol[:, :], 0.0, wr=[rMOD])
        rT = Res("tmpc")
        X("act", "activation", out=sm[:, 0:16], in_=ccol[:, :], func=AF.Tanh, scale=0.5, rd=[rC], wr=[rT])
        X("dve", "scalar_tensor_tensor", out=sm[:, 16:32], in0=sm[:, 0:16], scalar=1.0, in1=ccol[:, :],
          op0=ALU.add, op1=ALU.mult, rd=[rT, rC], wr=[rT])
        rCACT = Res("cact")
        X("dve", "tensor_scalar", out=cact[:, :], in0=sm[:, 16:32], scalar1=0.5, scalar2=None, op0=ALU.mult,
          rd=[rT], wr=[rCACT])
        rESK = Res("esk")
        X("act", "activation", out=esk[:, :], in_=esk[:, :], func=AF.Exp, rd=[rC], wr=[rESK])
        rLG = Res("lg")
        X("dve", "tensor_scalar", out=lgcol[:, :], in0=lgcol[:, :], scalar1=0.5, scalar2=None, op0=ALU.mult,
          rd=[rC], wr=[rLG])

        rowbuf = [rowt[:, 0:256], rowt[:, 256:512]]
        rROW = [Res("row0"), Res("row1")]
        gem_pending = []
        gem_count = [0]

        YS4 = YT[:, :].rearrange("p (s k n) -> p s k n", s=8, k=16)
        rYS = [Res(f"ys{i}") for i in range(8)]

        def gemv_chunk(ci, rowbank, colbank, startup=False):
            if len(gem_pending) >= 2:
                gem_flush(len(gem_pending) - 1)
            if startup:
                if ci < 8:
                    wt, rw = YS4[:, ci], rYS[ci]
                    src_ap = w_ada.rearrange("(k p) n -> p k n", p=128)[:, :, ci * 256:(ci + 1) * 256]
                    dma("pool", wt, src_ap, wr=[rw])
                elif ci < 12:
                    s = load_chunk(w_ada, ci * 256)
                    wt, rw = WB4[:, s], rWB[s]
                else:
                    wt, rw = YS4[:, ci - 12], rYS[ci - 12]
                    src_ap = w_ada.rearrange("(k p) n -> p k n", p=128)[:, :, ci * 256:(ci + 1) * 256]
                    dma("pool", wt, src_ap, wr=[rw])
            else:
                if ci < 24:
                    s = load_chunk(w_ada, ci * 256)
                else:
                    s = load_chunk(w_adaf, (ci - 24) * 256)
                wt, rw = WB4[:, s], rWB[s]
            rb = gem_count[0] % 2
            gem_count[0] += 1
            bk = bank(rowbank)
            for kt in range(16):
                MM(bk[0:1, 0:256], cact[:, kt:kt + 1], wt[:, kt, :], kt == 0, kt == 15,
                   [rw, rCACT], rBANK[rowbank], kt == 0)
            X("act", "activation", out=rowbuf[rb][0:1, :], in_=bk[0:1, 0:256], func=AF.Copy,
              rd=[rBANK[rowbank]], wr=[rROW[rb]])

            def tr():
                cb = bank(colbank)
                c0 = (ci % 8) * 2
                for j in range(2):
                    MM(cb[:, c0 + j:c0 + j + 1], rowbuf[rb][0:1, j * 128:(j + 1) * 128], onesf[0:1, 0:1], True, True,
                       [rROW[rb], rC], rBANK[colbank], j == 0)
                X("dve", "tensor_copy", out=modcol[:, 2 * ci:2 * ci + 2], in_=cb[:, c0:c0 + 2],
                  rd=[rBANK[colbank]], ac=[rMOD])
            gem_pending.append(tr)

        def gem_flush(n=None):
            k = 0
            while gem_pending and (n is None or k < n):
                gem_pending.pop(0)()
                k += 1

        xst = [carve(R, i * 2048, 2048) for i in range(3)]
        xnb = [carve(R, 6144 + i * 1024, 1024, BF16) for i in range(2)]
        rXST = [Res("xst0"), Res("xst1"), Res("xst2")]
        rXNB = [Res("xnb0"), Res("xnb1")]
        rSS = Res("ss")
        X("dve", "memset", sm[:, 32:96], 0.0, wr=[rSS])

        def xstage_a(blk):
            bi = blk % 2
            xi = blk % 3
            dma("sp", xst[xi], xs[blk * 128:(blk + 1) * 128, :], wr=[rXST[xi]])
            ssb = sm[:, 32 + blk:33 + blk]
            rs = sm[:, 49 + blk:50 + blk]
            X("act", "activation", out=xnb[bi], in_=xst[xi], func=AF.Square, accum_out=ssb,
              rd=[rXST[xi], rSS], wr=[rXNB[bi]], ac=[rSS])
            X("dve", "tensor_scalar", out=rs, in0=ssb, scalar1=1.0 / D, scalar2=EPS, op0=ALU.mult, op1=ALU.add,
              rd=[rSS], ac=[rSS])
            X("act", "activation", out=rs, in_=rs, func=AF.Sqrt, rd=[rSS], ac=[rSS])
            X("dve", "reciprocal", out=rs, in_=rs, rd=[rSS], ac=[rSS])
            X("dve", "tensor_scalar", out=xnb[bi], in0=xst[xi], scalar1=rs, scalar2=None, op0=ALU.mult,
              rd=[rXST[xi], rSS], wr=[rXNB[bi]])

        def xstage_b(blk):
            bi = blk % 2
            b0 = 2 * (blk % 2)
            for hb in range(2):
                bk = bankbf(b0 + hb)
                for j in range(8):
                    kt = hb * 8 + j
                    X("pe", "transpose", bk[:, j * 128:(j + 1) * 128], xnb[bi][:, kt * 128:(kt + 1) * 128], ident,
                      rd=[rXNB[bi], rC], wr=[rBANK[b0 + hb]] if j == 0 else (), ac=() if j == 0 else [rBANK[b0 + hb]])
                src = bk.rearrange("p (a b) -> p a b", a=8)
                if blk == 0:
                    dst = HTH3[:, hb * 8:(hb + 1) * 8, :]
                    acc = [rHTH]
                else:
                    dst = HT3[:, hb * 8:(hb + 1) * 8, (blk - 1) * 128:blk * 128]
                    acc = [rHT[k] for k in range(hb * 8, hb * 8 + 8)]
                if hb == 0:
                    X("act", "activation", out=dst, in_=src, func=AF.Copy, rd=[rBANK[b0 + hb]], ac=acc)
                else:
                    X("dve", "tensor_copy", out=dst, in_=src, rd=[rBANK[b0 + hb]], ac=acc)

        xstage_a(0)
        for blk in range(17):
            if blk + 1 < 17:
                xstage_a(blk + 1)
            if blk < 16:
                gemv_chunk(blk, 5 + (blk % 2), 7, startup=True)
            xstage_b(blk)
            if blk >= 1:
                gem_flush(1)
        gem_flush()
        rMODB = Res("modb")
        X("dve", "tensor_tensor", out=modcol[:, 0:32], in0=modcol[:, 0:32], in1=bcol[:, 0:32], op=ALU.add,
          rd=[rMOD, rC], wr=[rMODB])
        X("dve", "scalar_tensor_tensor", out=acol[:, :], in0=modcol[:, 16:32], scalar=1.0, in1=gcol[:, :],
          op0=ALU.add, op1=ALU.mult, rd=[rMODB, rC], wr=[rMODB, rSTART])
        S.barrier()
        for kt in range(16):
            X("dve", "tensor_scalar", out=HTH3[:, kt, :], in0=HTH3[:, kt, :], scalar1=acol[:, kt:kt + 1],
              scalar2=modcol[:, kt:kt + 1], op0=ALU.mult, op1=ALU.add, rd=[rMODB], wr=[rHTH])

        def affine_tg(tg):
            for kt in range(16):
                X("dve", "tensor_scalar", out=HT3[:, kt, tg * 512:(tg + 1) * 512], in0=HT3[:, kt, tg * 512:(tg + 1) * 512],
                  scalar1=acol[:, kt:kt + 1], scalar2=modcol[:, kt:kt + 1], op0=ALU.mult, op1=ALU.add,
                  rd=[rMODB, rHT[kt]], ac=[rHT[kt]])
        if os.environ.get("KVERB"):
            print("ops after startup", S.nadd)
        if STOP <= 1:
            S.muted = True

        KT = carve(R, 0, 1088, BF16)
        VAF = carve(R, 1088, 2176, BF16)
        VA = VAF.rearrange("p (b e c) -> p b e c", b=17, e=2)
        MSK = carve(R, 3264, 512, BF16).rearrange("p (w c) -> p w c", w=2)
        CSB = [[carve(R, 3776 + (2 * i + j) * 512, 512) for j in range(2)] for i in range(2)]
        QSB = [carve(R, 5824 + i * 256, 256, BF16) for i in range(2)]
        T1 = [carve(R, 6336 + i * 512, 512) for i in range(2)]
        T2 = [carve(R, 7360 + i * 512, 512) for i in range(2)]
        ES2 = carve(R, 8384, 512).rearrange("p (e c) -> p e c", e=2)
        pswap = carve(R, 8896, 64, BF16)
        identb = carve(R, 8960, 64, BF16)
        SEL = [carve(R, 9024 + i * 64, 64, BF16) for i in range(2)]
        ESR = [carve(R, 9152 + i * 128, 128, BF16) for i in range(2)]
        NQB = 3
        QT = [carve(YTLOW, i * 512, 512, BF16).rearrange("p (j t) -> p j t", j=2) for i in range(NQB)]
        GZ = [carve(YTLOW, 1536 + i * 1024, 1024).rearrange("p (j t) -> p j t", j=2) for i in range(NQB)]
        NPT = 6
        PT = [carve(YTLOW, 4608 + i * 256, 256, BF16) for i in range(NPT)]
        REC = [carve(YTLOW, 6144 + i * 256, 256) for i in range(2)]
        TT = [carve(YTLOW, 6656 + i * 256, 256) for i in range(2)]
        CSH = [carve(YTLOW, 7168 + i * 128, 128) for i in range(2)]
        SG = [carve(YTLOW, 7424, 512), carve(YTLOW, 7424, 512)]

        rKT = Res("kt")
        rVA = Res("va")
        rMSK = Res("msk")
        rCSB = [Res("csb0"), Res("csb1")]
        rQSB = [Res("qsb0"), Res("qsb1")]
        rT1 = [Res("t1a"), Res("t1b")]
        rT2 = [Res("t2a"), Res("t2b")]
        rES2 = Res("es2")
        rQT = [Res(f"qt{i}") for i in range(NQB)]
        rGZ = [Res(f"gz{i}") for i in range(NQB)]
        rPT = [Res(f"pt{i}") for i in range(NPT)]
        rSG = [Res("sg0")] * 2
        rREC = [Res("rec0"), Res("rec1")]
        rTT = [Res("tt0"), Res("tt1")]
        rCSH = Res("csh")
        rPSW = Res("pswap")

        dma("sp", MSK, masks_in[:, :].rearrange("p (w c) -> p w c", w=2), wr=[rMSK])
        dma("sp", pswap, pswap_in[:, :], wr=[rPSW])
        dma("sp", identb, ident_in[:, :], ac=[rPSW])
        rSEL = Res("sel")
        X("dve", "memset", SEL[0][0:1, 0:64], 0.0, wr=[rSEL])
        X("dve", "memset", SEL[0][0:1, 64:128], 1.0, ac=[rSEL])
        X("dve", "memset", SEL[1][0:1, 0:64], 1.0, ac=[rSEL])
        X("dve", "memset", SEL[1][0:1, 64:128], 0.0, ac=[rSEL])
        dma("sp", CSH[0], cos_in[:, 0:128], wr=[rCSH])
        dma("sp", CSH[1], sin_in[:, 0:128], ac=[rCSH])

        accb = [0]
        ropei = [0]
        SWAPB = 2
        PVB = 5

        def next_acc():
            b = accb[0] % 2
            accb[0] += 1
            return b

        def rope_unit(ab, n, cos_ap, sin_ap, rtab, dst_ap, dst_accum, swapbanks=(2, 2)):
            i = ropei[0] % 2
            ropei[0] += 1
            SWAPB = swapbanks[i]
            src = bank(ab)[:, 0:n]
            X("act", "activation", out=QSB[i][:, 0:n], in_=src, func=AF.Copy, rd=[rBANK[ab]], wr=[rQSB[i]])
            sw = bank(SWAPB)[:, 0:n]
            MM(sw, pswap, QSB[i][:, 0:n], True, True, [rQSB[i], rPSW], rBANK[SWAPB], True)
            X("dve", "tensor_tensor", out=T1[i][:, 0:n], in0=src, in1=cos_ap, op=ALU.mult,
              rd=[rBANK[ab]] + rtab, wr=[rT1[i]])
            X("dve", "tensor_tensor", out=T2[i][:, 0:n], in0=sw, in1=sin_ap, op=ALU.mult,
              rd=[rBANK[SWAPB]] + rtab, wr=[rT2[i]])
            X("dve", "tensor_tensor", out=dst_ap, in0=T1[i][:, 0:n], in1=T2[i][:, 0:n], op=ALU.add,
              rd=[rT1[i], rT2[i]], ac=dst_accum)

        def fm_matmuls(ab, s, c0, rhs_fn, rhs_res, kts=range(16)):
            for kt in kts:
                rhs = rhs_fn(kt)
                n = rhs.shape[-1]
                MM(bank(ab)[:, 0:n], WB4[:, s, kt, c0:c0 + 128], rhs, kt == 0, kt == 15,
                   [rWB[s], rhs_res[kt]], rBANK[ab], kt == 0)

        gem_next = [16]

        def gem_bg(n, rows=(6, 6), col=7):
            for i in range(n):
                if gem_next[0] < 40:
                    gemv_chunk(gem_next[0], rows[i % 2], col)
                    gem_next[0] += 1
                    if i % 2 == 1:
                        gem_flush()
            gem_flush()

        csbuf = [0]

        def load_cs(tg):
            cb = csbuf[0] % 2
            csbuf[0] += 1
            dma("sp", CSB[cb][0], cos_in[:, 128 + tg * 512:128 + (tg + 1) * 512], wr=[rCSB[cb]])
            dma("sp", CSB[cb][1], sin_in[:, 128 + tg * 512:128 + (tg + 1) * 512], ac=[rCSB[cb]])
            return cb

        LA = 4
        sgi = [0]
        for t in range(2):
            skv = load_chunk(w_in, (5 * t) * 256)
            X("dve", "memset", VAF, 1.0, wr=[rVA])
            ab = next_acc()
            fm_matmuls(ab, skv, 0, lambda kt: HTH3[:, kt, :], [rHTH] * 16)
            rope_unit(ab, 128, CSH[0], CSH[1], [rCSH], KT[:, 0:128], [rKT], (2, 3))
            for tg in range(4):
                if t == 0:
                    affine_tg(tg)
                cb = load_cs(tg)
                ab = next_acc()
                fm_matmuls(ab, skv, 0, lambda kt: HT3[:, kt, tg * 512:(tg + 1) * 512], rHT)
                rope_unit(ab, 512, CSB[cb][0], CSB[cb][1], [rCSB[cb]], KT[:, 128 + tg * 512:128 + (tg + 1) * 512], [rKT], (2, 3))
            for blk in range(17):
                ab = next_acc()
                for kt in range(16):
                    lhsT = HTH3[:, kt, :] if blk == 0 else HT3[:, kt, (blk - 1) * 128:blk * 128]
                    MM(bank(ab)[:, 0:128], lhsT, WB4[:, skv, kt, 128:256], kt == 0, kt == 15,
                       [rWB[skv], rHTH if blk == 0 else rHT[kt]], rBANK[ab], kt == 0)
                X("act", "activation", out=VA[:, blk, 0, 0:64], in_=bank(ab)[:, 0:64], func=AF.Copy, rd=[rBANK[ab]], ac=[rVA])
                X("dve", "tensor_copy", out=VA[:, blk, 1, 64:128], in_=bank(ab)[:, 64:128], rd=[rBANK[ab]], ac=[rVA])
            gem_bg(1, rows=(3, 4), col=5)
            for gp in range(2):
                sq = load_chunk(w_in, (5 * t + 1 + 2 * gp) * 256)
                sz = load_chunk(w_in, (5 * t + 2 + 2 * gp) * 256)
                for e_ in range(2):
                    for j in range(2):
                        h = 8 * t + 4 * e_ + 2 * gp + j
                        fst = (e_ == 0 and j == 0)
                        X("dve", "tensor_scalar", out=ESR[e_][0:1, j * 128:(j + 1) * 128], in0=onesf[0:1, :],
                          scalar1=esk[0:1, h:h + 1], scalar2=None, op0=ALU.mult,
                          rd=[rESK, rC], wr=[rES2] if fst else (), ac=() if fst else [rES2])

                def make_group(kind, j, tg, st):
                    qb = tg % NQB
                    box = {}
                    sw = sq if kind == "q" else sz

                    def h0():
                        if "cb" not in st:
                            st["cb"] = load_cs(tg)
                        box["ab"] = next_acc()
                        fm_matmuls(box["ab"], sw, j * 128, lambda kt: HT3[:, kt, tg * 512:(tg + 1) * 512], rHT, range(0, 8))

                    def h1():
                        ab = box["ab"]
                        cb = st["cb"]
                        fm_matmuls(ab, sw, j * 128, lambda kt: HT3[:, kt, tg * 512:(tg + 1) * 512], rHT, range(8, 16))
                        if kind == "q":
                            rope_unit(ab, 512, CSB[cb][0], CSB[cb][1], [rCSB[cb]], QT[qb][:, j, :], [rQT[qb]])
                        else:
                            si = sgi[0] % 2
                            sgi[0] += 1
                            X("act", "activation", out=SG[si], in_=bank(ab)[:, :], func=AF.Exp, scale=-1.0,
                              rd=[rBANK[ab]], wr=[rSG[si]])
                            X("act", "activation", out=SG[si], in_=SG[si], func=AF.Ln, bias=1.0, rd=[rSG[si]], wr=[rSG[si]])
                            X("act", "activation", out=SG[si], in_=SG[si], func=AF.Exp, scale=-1.0, rd=[rSG[si]], wr=[rSG[si]])
                            X("dve", "tensor_tensor", out=GZ[qb][:, j, :], in0=bank(ab)[:, :], in1=SG[si], op=ALU.mult,
                              rd=[rBANK[ab], rSG[si]], ac=[rGZ[qb]])
                    return [h0, h1]

                def make_G(tg):
                    st = {}
                    return (make_group("q", 0, tg, st) + make_group("z", 0, tg, st) +
                            make_group("q", 1, tg, st) + make_group("z", 1, tg, st))

                def S_unit(k):
                    tg, i = divmod(k, 8)
                    nb, e_ = divmod(i, 2)
                    n = 4 * tg + nb
                    qb = tg % NQB
                    scb = 3 + k % 3
                    pt = PT[k % NPT]
                    rpt = rPT[k % NPT]
                    psl = slice(e_ * 64, (e_ + 1) * 64)
                    mi = 1 if n == 0 else 0
                    MM(bank(scb)[:, :], identb, MSK[:, mi, :], True, False, [rMSK, rPSW], rBANK[scb], True)
                    for w in range(2):
                        k0 = (n + 1 - w) * 128
                        MM(bank(scb)[:, w * 256:(w + 1) * 256].rearrange("p (j q) -> p j q", j=2),
                           KT[psl, k0:k0 + 128], QT[qb][psl, :, nb * 128:(nb + 1) * 128], False, (w == 1),
                           [rKT, rQT[qb]], rBANK[scb], False)
                    X("act", "activation", out=pt, in_=bank(scb)[:, :], func=AF.Exp, scale=0.125, rd=[rBANK[scb]], wr=[rpt])

                def PV_unit(k):
                    tg, i = divmod(k, 8)
                    nb, e_ = divmod(i, 2)
                    n = 4 * tg + nb
                    qb = tg % NQB
                    pt = PT[k % NPT]
                    rpt = rPT[k % NPT]
                    u = k % 2
                    pvb = 6 + u
                    psl = slice(e_ * 64, (e_ + 1) * 64)
                    dsl = slice((1 - e_) * 64, (2 - e_) * 64)
                    MM(bank(pvb)[:, 0:256], VA[:, n, e_, :], pt[:, 256:512], True, False, [rVA, rpt], rBANK[pvb], True)
                    MM(bank(pvb)[:, 0:256], VA[:, n + 1, e_, :], pt[:, 0:256], False, True, [rVA, rpt], rBANK[pvb], False)
                    for j in range(2):
                        h = 8 * t + 4 * e_ + 2 * gp + j
                        X("act", "activation", out=REC[u][psl, j * 128:(j + 1) * 128], in_=bank(pvb)[dsl, j * 128:(j + 1) * 128],
                          func=AF.Ln, bias=esk[dsl, h:h + 1], rd=[rBANK[pvb], rESK],
                          wr=[rREC[u]] if j == 0 else (), ac=() if j == 0 else [rREC[u]])
                    X("act", "activation", out=REC[u][psl, :], in_=REC[u][psl, :], func=AF.Exp, scale=-1.0, rd=[rREC[u]], wr=[rREC[u]])
                    X("dve", "tensor_tensor", out=TT[u][psl, :], in0=bank(pvb)[psl, 0:256], in1=REC[u][psl, :],
                      op=ALU.mult, rd=[rBANK[pvb], rREC[u]], wr=[rTT[u]])
                    k0t = 8 + 4 * t + 2 * gp
                    X("dve", "tensor_tensor", out=YT3[psl, k0t:k0t + 2, n * 128:(n + 1) * 128],
                      in0=TT[u][psl, :].rearrange("p (j q) -> p j q", j=2),
                      in1=GZ[qb][psl, :, nb * 128:(nb + 1) * 128], op=ALU.mult,
                      rd=[rTT[u], rGZ[qb]], ac=[rYT[k0t], rYT[k0t + 1]])

                for f in make_G(0):
                    f()
                for tg in range(4):
                    nxt = make_G(tg + 1) if tg < 3 else []
                    for i in range(8):
                        k = 8 * tg + i
                        S_unit(k)
                        if k - LA >= 0:
                            PV_unit(k - LA)
                        for _ in range(2 if i == 0 else (1 if i < 7 else 0)):
                            if nxt:
                                nxt.pop(0)()
                for k in range(32 - LA, 32):
                    PV_unit(k)
                gem_bg(2, rows=(3, 4), col=5)
        gem_flush()
        S.barrier()
        if os.environ.get("KVERB"):
            print("ops after B", S.nadd)
        if STOP <= 2:
            S.muted = True

        VH = carve(R, 0, 8192, BF16).rearrange("p (b c) -> p b c", b=16)
        CP = carve(R, 8192, 1024).rearrange("p (g t) -> p g t", g=8)
        WSB = carve(R, 9216, 512, BF16).rearrange("p (g t) -> p g t", g=8)
        STT_ = carve(R, 9728, 192).rearrange("p (b h s) -> p b h s", b=16, h=2)
        STF = carve(R, 9728, 192)
        MV = carve(R, 9920, 32).rearrange("p (b s) -> p b s", b=16)
        RS = carve(R, 9952, 16)
        NMR = carve(R, 9968, 16)
        WSF = carve(YTLOW, 0, 1024).rearrange("p (g t) -> p g t", g=8)
        RSUM = carve(YTLOW, 1024, 1024)
        LBR = carve(YTLOW, 2048, 1024)
        BSR = carve(YTLOW, 3072, 1024)
        TRI = carve(YTLOW, 4096, 128)
        rVH = [Res(f"vh{b}") for b in range(16)]
        rCP = Res("cp")
        rWSB = Res("wsb")
        rWSF = Res("wsf")
        rST = Res("st")
        rRS = Res("rs")
        rRSUM = Res("rsum")
        rSET = Res("setA")

        dma("sp", WSF, wst_in[:, :].rearrange("p (g t) -> p g t", g=8), wr=[rWSF])
        dma("sp", TRI, tri_in[:, :], wr=[rSET])
        dma("sp", LBR[0:1, :], lb_row[:, :], ac=[rSET])
        dma("sp", BSR[0:1, :], bs_row[:, :], ac=[rSET])
        X("dve", "tensor_tensor", out=WSF, in0=WSF, in1=TRI.unsqueeze(1).broadcast_to([128, 8, 128]), op=ALU.mult,
          rd=[rSET], wr=[rWSF])
        X("dve", "tensor_copy", out=WSB, in_=WSF, rd=[rWSF], wr=[rWSB])
        for g in range(8):
            bb = 2 if g < 4 else 3
            oc = (g % 4) * 128
            MM(bank(bb)[0:1, oc:oc + 128], onesf[:, 0:1], WSF[:, g, :], True, True, [rWSF, rC], rBANK[bb], g % 4 == 0)
        X("act", "activation", out=RSUM[0:1, 0:512], in_=bank(2)[0:1, :], func=AF.Copy, rd=[rBANK[2]], wr=[rRSUM])
        X("act", "activation", out=RSUM[0:1, 512:1024], in_=bank(3)[0:1, :], func=AF.Copy, rd=[rBANK[3]], ac=[rRSUM])
        for g in range(8):
            bb = 2 if g < 4 else 3
            oc = (g % 4) * 128
            MM(bank(bb)[:, oc:oc + 128], LBR[0:1, g * 128:(g + 1) * 128], RSUM[0:1, g * 128:(g + 1) * 128], True, False,
               [rRSUM, rSET], rBANK[bb], g % 4 == 0)
            MM(bank(bb)[:, oc:oc + 128], onesf[0:1, :], BSR[0:1, g * 128:(g + 1) * 128], False, True,
               [rSET, rC], rBANK[bb], False)
        X("act", "activation", out=CP[:, 0:4, :], in_=bank(2)[:, :].rearrange("p (g t) -> p g t", g=4), func=AF.Identity,
          scale=0.5, rd=[rBANK[2]], wr=[rCP])
        X("act", "activation", out=CP[:, 4:8, :], in_=bank(3)[:, :].rearrange("p (g t) -> p g t", g=4), func=AF.Identity,
          scale=0.5, rd=[rBANK[3]], ac=[rCP])

        X("dve", "memset", STF, 0.0, wr=[rST])
        vslots = [load_chunk(w_in, (10 + i) * 256) for i in range(4)]
        vacc = [0]
        for hh in range(2):
            s0, s1 = vslots[2 * hh], vslots[2 * hh + 1]
            for blk in range(16):
                ab = vacc[0] % 4
                vacc[0] += 1
                assert s1 == s0 + 1
                for kt in range(16):
                    MM(bank(ab)[:, :].rearrange("p (a b) -> p a b", a=2), HT3[:, kt, blk * 128:(blk + 1) * 128],
                       WB4[:, s0:s0 + 2, kt, :], kt == 0, kt == 15, [rWB[s0], rWB[s1], rHT[kt]], rBANK[ab], kt == 0)
                X("dve", "bn_stats", out=STT_[:, blk, hh, :], in_=bank(ab)[:, :], rd=[rBANK[ab]], ac=[rST])
                X("act", "activation", out=VH[:, blk, hh * 512:(hh + 1) * 512], in_=bank(ab)[:, :], func=AF.Copy,
                  rd=[rBANK[ab]], ac=[rVH[blk]])
            gem_bg(2)
        gem_flush()
        rMV = Res("mv")
        for blk in range(16):
            X("dve", "bn_aggr", out=MV[:, blk, :], in_=STT_[:, blk, :, :].rearrange("p h s -> p (h s)"),
              rd=[rST], wr=[rMV] if blk == 0 else (), ac=() if blk == 0 else [rMV])
        X("dve", "tensor_scalar", out=RS, in0=MV[:, :, 1], scalar1=EPS, scalar2=None, op0=ALU.add, rd=[rMV], wr=[rRS])
        X("act", "activation", out=RS, in_=RS, func=AF.Sqrt, rd=[rRS], wr=[rRS])
        X("dve", "reciprocal", out=RS, in_=RS, rd=[rRS], wr=[rRS])
        X("dve", "scalar_tensor_tensor", out=NMR, in0=MV[:, :, 0], scalar=-1.0, in1=RS, op0=ALU.mult, op1=ALU.mult,
          rd=[rRS, rMV], wr=[rRS])
        for blk in range(16):
            X("dve", "tensor_scalar", out=VH[:, blk, :], in0=VH[:, blk, :], scalar1=RS[:, blk:blk + 1],
              scalar2=NMR[:, blk:blk + 1], op0=ALU.mult, op1=ALU.add, rd=[rRS], wr=[rVH[blk]])
        if STOP <= 3:
            S.muted = True

        S1 = carve(HTHW, 0, 512)
        TH = carve(HTHW, 512, 512)
        rS1 = Res("s1")
        rTH = Res("th")
        unit = [0]
        for g in range(8):
            sg = load_chunk(w_in, (14 + g) * 256)
            if g % 2 == 0:
                qd = g // 2
                dma("pool", wo_bf[qd * 512:(qd + 1) * 512, :], w_out[qd * 512:(qd + 1) * 512, :],
                    wr=[rWOBF] if qd == 0 else (), ac=() if qd == 0 else [rWOBF])
            for tg in range(4):
                b0 = 3 * (unit[0] % 2)
                unit[0] += 1
                bU, bZ, bA = b0, b0 + 1, b0 + 2
                fm_matmuls(bU, sg, 0, lambda kt: HT3[:, kt, tg * 512:(tg + 1) * 512], rHT)
                fm_matmuls(bZ, sg, 128, lambda kt: HT3[:, kt, tg * 512:(tg + 1) * 512], rHT)
                for c in range(4):
                    blk = 4 * tg + c
                    MM(bank(bA)[:, c * 128:(c + 1) * 128], VH[:, blk, g * 128:(g + 1) * 128], WSB[:, g, :], True, True,
                       [rVH[blk], rWSB], rBANK[bA], c == 0)
                X("dve", "scalar_tensor_tensor", out=S1.rearrange("p (c t) -> p c t", c=4),
                  in0=bank(bA)[:, :].rearrange("p (c t) -> p c t", c=4), scalar=lgcol[:, g:g + 1],
                  in1=CP[:, g, :].unsqueeze(1).broadcast_to([128, 4, 128]), op0=ALU.mult, op1=ALU.add,
                  rd=[rBANK[bA], rCP, rLG], wr=[rS1])
                X("dve", "tensor_tensor", out=S1, in0=S1, in1=bank(bU)[:, :], op=ALU.mult, rd=[rBANK[bU]], wr=[rS1])
                X("act", "activation", out=TH, in_=bank(bZ)[:, :], func=AF.Tanh, scale=0.5, rd=[rBANK[bZ]], wr=[rTH])
                X("dve", "scalar_tensor_tensor", out=TH, in0=TH, scalar=1.0, in1=bank(bZ)[:, :], op0=ALU.add, op1=ALU.mult,
                  rd=[rBANK[bZ]], wr=[rTH])
                X("dve", "tensor_tensor", out=YT3[:, g, tg * 512:(tg + 1) * 512], in0=S1, in1=TH, op=ALU.mult,
                  rd=[rS1, rTH], ac=[rYT[g]])
                gem_flush()
            gem_bg(1 if g < 7 else 3)
        gem_flush()
        S.barrier()

        if STOP <= 4:
            S.muted = True
        if DBG:
            dma("sp", dbg_yt[:, :], YT[:, :], rd=rYT)
            dma("sp", dbg_ht[:, :], HT[:, :], rd=rHT)
            S.barrier()
        X("dve", "tensor_tensor", out=modcol[:, 32:80], in0=modcol[:, 32:80], in1=bcol[:, 32:80], op=ALU.add,
          rd=[rMOD, rC], wr=[rMODB])
        X("dve", "scalar_tensor_tensor", out=afcol[:, :], in0=modcol[:, 64:80], scalar=1.0, in1=gfcol[:, :],
          op0=ALU.add, op1=ALU.mult, rd=[rMODB, rC], wr=[rMODB])
        XR = [carve(R, i * 2048, 2048) for i in range(2)] + [carve(WBW, 6144, 2048)]
        OUTB = [carve(R, 4096 + i * 2048, 2048) for i in range(2)]
        identf = carve(R, 8192, 128)
        TMP = [carve(R, 8320 + i * 512, 512) for i in range(2)]
        GB = carve(WBW, 0, 2048)
        AFBT = carve(WBW, 2048, 2048)
        SFB = carve(WBW, 4096, 2048)
        DG = [carve(HTHW, i * 128, 128) for i in range(2)]
        WO3 = HT3
        rXR = [Res("xr0"), Res("xr1"), Res("xr2")]
        rOUTB = [Res("ob0"), Res("ob1")]
        rTMP = [Res("tmp0"), Res("tmp1")]
        rGB = Res("gb")
        rAFB = Res("afb")
        rSFB = Res("sfb")
        rDG = [Res("dg0"), Res("dg1")]
        rSS2 = Res("ss2")
        rIDF = Res("identf")
        dma("sp", identf, identf_in[:, :], wr=[rIDF])
        for kt in range(16):
            dma("sp", WO3[:, kt, :], wo_bf[kt * 128:(kt + 1) * 128, :], rd=[rWOBF], wr=[rHT[kt]])

        def bcast_build(colap, dst, rdst):
            for c in range(16):
                i = c % 2
                X("dve", "tensor_scalar", out=DG[i], in0=identf, scalar1=colap[:, c:c + 1], scalar2=None, op0=ALU.mult,
                  rd=[rMODB, rIDF], wr=[rDG[i]])
                bb = (c // 4) % 2
                MM(bank(bb)[:, (c % 4) * 128:(c % 4 + 1) * 128], onesf[:, :], DG[i], True, True, [rDG[i], rC], rBANK[bb], c % 4 == 0)
                if c % 4 == 3:
                    q4 = c // 4
                    X("act", "activation", out=dst[:, q4 * 512:(q4 + 1) * 512], in_=bank(bb)[:, :], func=AF.Copy,
                      rd=[rBANK[bb]], wr=[rdst] if q4 == 0 else (), ac=() if q4 == 0 else [rdst])

        bcast_build(modcol[:, 32:48], GB, rGB)
        bcast_build(afcol, AFBT, rAFB)
        bcast_build(modcol[:, 48:64], SFB, rSFB)
        X("dve", "memset", sm[:, 32:96], 0.0, wr=[rSS2])
        ti = [0]
        pending_final = []
        for blk in range(16):
            i = blk % 3
            o = blk % 2
            dma("sp", XR[i], xs[(blk + 1) * 128:(blk + 2) * 128, :], wr=[rXR[i]])
            b0 = 4 * (blk % 2)
            for cg in range(4):
                for kt in range(16):
                    MM(bank(b0 + cg)[:, :], YT3[:, kt, blk * 128:(blk + 1) * 128], WO3[:, kt, cg * 512:(cg + 1) * 512],
                       kt == 0, kt == 15, [rYT[kt], rHT[kt]], rBANK[b0 + cg], kt == 0)
                tb = ti[0] % 2
                ti[0] += 1
                X("dve", "tensor_tensor", out=TMP[tb], in0=bank(b0 + cg)[:, :], in1=GB[:, cg * 512:(cg + 1) * 512], op=ALU.mult,
                  rd=[rBANK[b0 + cg], rGB], wr=[rTMP[tb]])
                X("dve", "tensor_tensor", out=XR[i][:, cg * 512:(cg + 1) * 512], in0=TMP[tb],
                  in1=XR[i][:, cg * 512:(cg + 1) * 512], op=ALU.add, rd=[rTMP[tb], rXR[i]], ac=[rXR[i]])
            ssb = sm[:, 32 + blk:33 + blk]
            rsb = sm[:, 49 + blk:50 + blk]
            if pending_final:
                pending_final.pop(0)()
            X("act", "activation", out=OUTB[o], in_=XR[i], func=AF.Square, accum_out=ssb,
              rd=[rXR[i], rSS2], wr=[rOUTB[o]], ac=[rSS2])

            def final_ops(blk=blk, i=i, o=o, ssb=ssb, rsb=rsb):
                X("dve", "tensor_scalar", out=rsb, in0=ssb, scalar1=1.0 / D, scalar2=EPS, op0=ALU.mult, op1=ALU.add,
                  rd=[rSS2], ac=[rSS2])
                X("act", "activation", out=rsb, in_=rsb, func=AF.Sqrt, rd=[rSS2], ac=[rSS2])
                X("dve", "reciprocal", out=rsb, in_=rsb, rd=[rSS2], ac=[rSS2])
                X("dve", "scalar_tensor_tensor", out=OUTB[o], in0=XR[i], scalar=rsb, in1=AFBT, op0=ALU.mult, op1=ALU.mult,
                  rd=[rXR[i], rSS2, rAFB], wr=[rOUTB[o]])
                X("dve", "tensor_tensor", out=OUTB[o], in0=OUTB[o], in1=SFB, op=ALU.add, rd=[rSFB], wr=[rOUTB[o]])
                dma("pool", out[blk * 128:(blk + 1) * 128, :], OUTB[o], rd=[rOUTB[o]])
            pending_final.append(final_ops)
        while pending_final:
            pending_final.pop(0)()
        if DBG:
            dma("sp", dbg_mod[:, :], modcol[:, :], rd=[rMODB])
        S.muted = False
        S.force = True
        S.barrier()
        S.add("sp", lambda e: e.nop())

        if os.environ.get("KTRACE"):
            lo, hi = [int(v) for v in os.environ["KTRACE"].split(",")]
            for (i, e, l) in S.trace_lbl:
                if lo <= i <= hi:
                    print(i, e, l)
        S.finalize()
        block = es.enter_context(nc.Block())

        @block.tensor
        def _(e):
            S.emit("pe", e, esem, dsems)

        @block.scalar
        def _(e):
            S.emit("act", e, esem, dsems)

        @block.vector
        def _(e):
            S.emit("dve", e, esem, dsems)

        @block.gpsimd
        def _(e):
            S.emit("pool", e, esem, dsems)

        @block.sync
        def _(e):
            S.emit("sp", e, esem, dsems)
    return nc


def _col(v, n):
    return np.ascontiguousarray(np.asarray(v, np.float32).reshape(n, 128).T)


def _prep_shared(w_ada, b_ada, norm_g, w_in, ln_v_g, ln_v_b, w_spatial, b_spatial, sinks, w_out,
                 w_ada_final, b_ada_final, final_norm_g):
    f32 = np.float32
    w_in0 = np.asarray(w_in[0], f32)
    cols = []
    for t in range(2):
        kv = []
        for m in (2 * t, 2 * t + 1):
            kv.append(np.arange(4096 + m * 64, 4096 + (m + 1) * 64))
        for m in (2 * t, 2 * t + 1):
            kv.append(np.arange(4352 + m * 64, 4352 + (m + 1) * 64))
        cols.append(np.concatenate(kv))
        for gp in range(2):
            for base in (3072, 4608):
                cc = []
                for j in range(2):
                    g = 2 * gp + j
                    for h in (8 * t + g, 8 * t + 4 + g):
                        cc.append(np.arange(base + h * 64, base + (h + 1) * 64))
                cols.append(np.concatenate(cc))
    cols.append(np.arange(1024, 2048))
    for g in range(8):
        cols.append(np.arange(g * 128, (g + 1) * 128))
        cols.append(np.arange(2048 + g * 128, 2048 + (g + 1) * 128))
    perm = np.concatenate(cols)
    assert perm.shape[0] == 5632 and np.unique(perm).shape[0] == 5632
    w_in_p = np.ascontiguousarray(w_in0[:, perm])
    w_out0 = np.asarray(w_out[0], f32)
    rows = [np.arange(0, 1024)]
    for t in range(2):
        for g in range(4):
            for h in (8 * t + g, 8 * t + 4 + g):
                rows.append(np.arange(1024 + h * 64, 1024 + (h + 1) * 64))
    rperm = np.concatenate(rows)
    assert rperm.shape[0] == 2048 and np.unique(rperm).shape[0] == 2048
    w_out_p = np.ascontiguousarray(w_out0[rperm, :])
    b_col = np.concatenate([_col(b_ada[0], 48), _col(b_ada_final, 32)], axis=1)
    wst = np.ascontiguousarray(np.transpose(np.asarray(w_spatial[0], f32), (2, 0, 1)).reshape(128, 8 * 128))
    tri = (np.arange(128)[:, None] <= np.arange(128)[None, :]).astype(f32)
    ident = np.eye(128, dtype=f32)
    pswap = np.zeros((128, 128), f32)
    for base in (0, 64):
        for m in range(32):
            pswap[base + m + 32, base + m] = -1.0
            pswap[base + m, base + m + 32] = 1.0
    shared = {
        "w_ada": np.ascontiguousarray(np.asarray(w_ada[0], f32)),
        "w_adaf": np.ascontiguousarray(np.asarray(w_ada_final, f32)),
        "b_col": np.ascontiguousarray(b_col),
        "g_col": _col(norm_g[0], 16),
        "gf_col": _col(final_norm_g, 16),
        "w_in": w_in_p,
        "w_out": w_out_p,
        "lg_col": _col(ln_v_g[0], 8),
        "lb_row": np.ascontiguousarray(np.asarray(ln_v_b[0], f32).reshape(1, 1024)),
        "wst": wst,
        "bs_row": np.ascontiguousarray(np.asarray(b_spatial[0], f32).reshape(1, 1024)),
        "tri": tri,
        "sinks_b": np.ascontiguousarray(np.broadcast_to(np.asarray(sinks[0], f32).reshape(1, 16), (128, 16))),
        "ident": ident.astype(ml_dtypes.bfloat16),
        "identf": ident,
        "pswap": pswap.astype(ml_dtypes.bfloat16),
    }
    return shared


def _rope_tables(half):
    f32 = np.float32
    pos_i = np.arange(17 * 128, dtype=np.int64) + half * 2048 - 128
    cos = sin = None
    try:
        import jax
        import jax.numpy as jnp
        cpu = jax.devices("cpu")[0]
        with jax.default_device(cpu):
            inv_freq_j = 10000.0 ** (-jnp.arange(0, 64, 2, dtype=jnp.float32) / 64)
            ang_j = jnp.asarray(pos_i.astype(f32))[:, None] * inv_freq_j[None, :]
            cos = np.asarray(jnp.cos(ang_j), dtype=f32).T
            sin = np.asarray(jnp.sin(ang_j), dtype=f32).T
    except Exception:
        cos = sin = None
    if cos is None:
        inv_freq = (np.float32(10000.0) ** (-(np.arange(0, 64, 2, dtype=f32)) / np.float32(64))).astype(f32)
        ang = (pos_i.astype(f32)[:, None] * inv_freq[None, :]).astype(f32)
        cos = np.cos(ang).astype(f32).T
        sin = np.sin(ang).astype(f32).T
    cosT = np.ascontiguousarray(np.tile(cos, (4, 1)))
    sinT = np.ascontiguousarray(np.tile(sin, (4, 1)))
    return cosT, sinT


def _masks(half):
    j = np.arange(128)[:, None]
    q = np.arange(128)[None, :]
    cur = (j <= q).astype(np.float32)
    prev = (j > q).astype(np.float32)
    pf = prev if half == 1 else np.zeros_like(prev)
    m0 = np.concatenate([cur, cur, prev, prev], axis=1)
    m1 = np.concatenate([cur, cur, pf, pf], axis=1)
    valid = np.concatenate([m0, m1], axis=1)
    bias = np.where(valid > 0.5, np.float32(0.0), np.float32(-30000.0)).astype(np.float32)
    return np.ascontiguousarray(bias).astype(ml_dtypes.bfloat16)


_NC_CACHE = {}


def kernel(x, c, w_ada, b_ada, norm_g, w_in, ln_v_g, ln_v_b, w_spatial, b_spatial, sinks, w_out,
           w_ada_final, b_ada_final, final_norm_g):
    x = np.asarray(x, np.float32)
    c = np.asarray(c, np.float32)
    shared = _prep_shared(w_ada, b_ada, norm_g, w_in, ln_v_g, ln_v_b, w_spatial, b_spatial, sinks, w_out,
                          w_ada_final, b_ada_final, final_norm_g)
    in_maps = []
    for core in range(NCORES):
        b, half = core // 2, core % 2
        own = x[b, half * 2048:(half + 1) * 2048]
        halo = x[b, 1920:2048] if half == 1 else np.zeros((128, D), np.float32)
        cosT, sinT = _rope_tables(half)
        m = dict(shared)
        m["xs"] = np.ascontiguousarray(np.concatenate([halo, own], axis=0))
        m["c_col"] = _col(c[b], 16)
        m["masks"] = _masks(half)
        m["cosT"] = cosT
        m["sinT"] = sinT
        in_maps.append(m)
    if "nc" not in _NC_CACHE:
        _NC_CACHE["nc"] = build_nc()
    nc = _NC_CACHE["nc"]
    res = run_bass_kernel_spmd(nc, in_maps, core_ids=list(range(NCORES)))
    outp = np.empty((4, 4096, D), np.float32)
    for core in range(NCORES):
        b, half = core // 2, core % 2
        outp[b, half * 2048:(half + 1) * 2048] = np.asarray(res.results[core]["out"], np.float32)
    return outp
```

```python
import os
import numpy as np
import ml_dtypes
from contextlib import ExitStack
import concourse.bass as bass
import concourse.mybir as mybir
from concourse.bass_utils import run_bass_kernel_spmd

F32 = mybir.dt.float32
BF16 = mybir.dt.bfloat16
AF = mybir.ActivationFunctionType
ALU = mybir.AluOpType

NCORES = 8
D = 2048
TOK = 2048
NB = 16
EPS = 1e-5
NCH_IN = 22
KDMA = 8


class Res:
    __slots__ = ("name", "writers", "readers", "excl")

    def __init__(self, name, excl=False):
        self.name = name
        self.writers = []
        self.readers = []
        self.excl = excl


class Op:
    __slots__ = ("eng", "fn", "deps", "sig", "dma", "dma_idx", "cnt")


class Sched:
    ENGS = ("pe", "act", "dve", "pool", "sp")

    def __init__(self):
        self.ops = {e: [] for e in self.ENGS}
        self.ndma = {e: 0 for e in self.ENGS}
        self.dma_ops = {e: [] for e in self.ENGS}
        self.floor = {}
        self.muted = False
        self.nadd = 0
        self.maxops = int(os.environ.get("KMAXOPS", "100000000"))

    def add(self, eng, fn, reads=(), writes=(), accum=(), dma=False):
        if self.muted:
            return None
        self.nadd += 1
        if getattr(self, "trace_lbl", None) is not None:
            self.trace_lbl.append((self.nadd, eng, getattr(self, "cur_lbl", "?")))
        if self.nadd > self.maxops and not getattr(self, "force", False):
            return None
        lst = self.ops[eng]
        idx = len(lst)
        deps = dict(self.floor.get(eng, {}))
        if eng in self.floor:
            del self.floor[eng]

        def need(h):
            e, i = h
            if e == eng and i >= idx:
                return
            if e == "pe" and eng == "pe":
                return
            key = (e, i) if self.ops[e][i].dma else (e, None)
            if key[1] is None:
                deps[key] = max(deps.get(key, -1), i)
            else:
                deps[key] = i

        for r in reads:
            for h in r.writers:
                need(h)
            if r.excl:
                for h in r.readers:
                    if h[0] != eng:
                        need(h)
        for r in writes:
            for h in r.writers:
                need(h)
            for h in r.readers:
                need(h)
        for r in accum:
            for h in r.readers:
                need(h)
            if r.writers:
                need(r.writers[0])
        op = Op()
        op.eng = eng
        op.fn = fn
        op.deps = [(k[0], v) for k, v in deps.items()]
        op.sig = False
        op.dma = dma
        op.dma_idx = None
        op.cnt = None
        if dma:
            op.dma_idx = self.ndma[eng]
            self.ndma[eng] += 1
            self.dma_ops[eng].append(idx)
            j = op.dma_idx
            if j >= KDMA:
                op.deps.append((eng, self.dma_ops[eng][j - KDMA]))
        lst.append(op)
        h = (eng, idx)
        for r in reads:
            r.readers.append(h)
        for r in writes:
            r.writers = [h]
            r.readers = []
        for r in accum:
            r.writers.append(h)
        return h

    def barrier(self):
        fl = {}
        for e in self.ENGS:
            n = len(self.ops[e])
            if n == 0:
                continue
            for i in range(n - 1, -1, -1):
                if not self.ops[e][i].dma:
                    fl[(e, None)] = i
                    break
            for i in self.dma_ops[e][-KDMA:]:
                fl[(e, i)] = i
        for e in self.ENGS:
            if e == "pool":
                continue
            self.floor[e] = dict(fl)

    def finalize(self):
        for e in self.ENGS:
            for op in self.ops[e]:
                for (de, di) in op.deps:
                    t = self.ops[de][di]
                    if not t.dma:
                        t.sig = True
        for e in self.ENGS:
            c = 0
            for op in self.ops[e]:
                if (not op.dma) and op.sig:
                    c += 1
                op.cnt = c

    def emit(self, ename, engobj, esem, dsems):
        waited = {}
        for op in self.ops[ename]:
            for (de, di) in op.deps:
                t = self.ops[de][di]
                if t.dma:
                    sem = dsems[de][t.dma_idx % KDMA]
                    val = 16 * (t.dma_idx // KDMA + 1)
                else:
                    sem = esem[de]
                    val = t.cnt
                key = id(sem)
                if waited.get(key, 0) < val:
                    engobj.wait_ge(sem, val)
                    waited[key] = val
            inst = op.fn(engobj)
            if op.dma:
                inst.then_inc(dsems[ename][op.dma_idx % KDMA], 16)
            elif op.sig:
                inst.then_inc(esem[ename], 1)
        return waited


def build_nc():
    nc = bass.Bass("TRN2", target_bir_lowering=False)

    def din(name, shape, dt=F32):
        return nc.dram_tensor(name, list(shape), dt, kind="ExternalInput").ap()

    xs = din("xs", [17 * 128, D])
    c_col = din("c_col", [128, 16])
    w_ada = din("w_ada", [D, 6144])
    w_adaf = din("w_adaf", [D, 4096])
    b_col = din("b_col", [128, 80])
    g_col = din("g_col", [128, 16])
    gf_col = din("gf_col", [128, 16])
    w_in = din("w_in", [D, 5632])
    w_out = din("w_out", [D, D])
    lg_col = din("lg_col", [128, 8])
    lb_row = din("lb_row", [1, 1024])
    wst_in = din("wst", [128, 8 * 128])
    bs_row = din("bs_row", [1, 1024])
    tri_in = din("tri", [128, 128])
    sinks_b = din("sinks_b", [128, 16])
    ident_in = din("ident", [128, 128], BF16)
    identf_in = din("identf", [128, 128])
    pswap_in = din("pswap", [128, 128], BF16)
    masks_in = din("masks", [128, 2 * 512], BF16)
    cos_in = din("cosT", [128, 17 * 128])
    sin_in = din("sinT", [128, 17 * 128])
    out = nc.dram_tensor("out", [TOK, D], F32, kind="ExternalOutput").ap()
    wo_bf = nc.dram_tensor("wo_bf", [D, D], BF16, kind="Internal").ap()
    DBG = os.environ.get("KDBG") == "1"
    STOP = int(os.environ.get("KSTOP", "99"))
    if DBG:
        dbg_mod = nc.dram_tensor("dbg_mod", [128, 80], F32, kind="ExternalOutput").ap()
        dbg_yt = nc.dram_tensor("dbg_yt", [128, 16 * 2048], BF16, kind="ExternalOutput").ap()
        dbg_ht = nc.dram_tensor("dbg_ht", [128, 16 * 2048], BF16, kind="ExternalOutput").ap()

    S = Sched()
    if os.environ.get("KTRACE"):
        S.trace_lbl = []
    es = ExitStack()
    with es:
        es.enter_context(nc.allow_low_precision("bf16 matmul operands, fp32 accumulation"))

        def sb(name, shape, dt):
            return es.enter_context(nc.sbuf_tensor(name, list(shape), dt))

        HT = sb("HT", [128, 16 * 2048], BF16)
        HTH = sb("HTH", [128, 16 * 128], BF16)
        YT = sb("YT", [128, 16 * 2048], BF16)
        WB = sb("WB", [128, 4 * 16 * 256], BF16)
        R = sb("R", [128, 9984], F32)
        onesf = sb("onesf", [128, 128], F32)
        rowt = sb("rowt", [128, 512], F32)
        ccol = sb("ccol", [128, 16], F32)
        cact = sb("cact", [128, 16], BF16)
        modcol = sb("modcol", [128, 80], F32)
        bcol = sb("bcol", [128, 80], F32)
        gcol = sb("gcol", [128, 16], F32)
        gfcol = sb("gfcol", [128, 16], F32)
        acol = sb("acol", [128, 16], F32)
        afcol = sb("afcol", [128, 16], F32)
        lgcol = sb("lgcol", [128, 8], F32)
        esk = sb("esk", [128, 16], F32)
        sm = sb("sm", [128, 96], F32)
        PS = es.enter_context(nc.psum_tensor("PS", [128, 4096], F32))

        HT3 = HT[:, :].rearrange("p (k t) -> p k t", k=16)
        HTH3 = HTH[:, :].rearrange("p (k t) -> p k t", k=16)
        YT3 = YT[:, :].rearrange("p (k t) -> p k t", k=16)
        WB4 = WB[:, :].rearrange("p (s k n) -> p s k n", s=4, k=16)

        def bank(b):
            return PS[:, b * 512:(b + 1) * 512]

        def bankbf(b):
            return PS[:, b * 512:(b + 1) * 512].bitcast(BF16)

        def carve(region, off, words, dt=F32):
            ap = region[:, off:off + words]
            if dt is BF16:
                ap = ap.bitcast(BF16)
            return ap

        YTLOW = YT[:, 0:8 * 2048].bitcast(F32)
        HTHW = HTH[:, :].bitcast(F32)
        WBW = WB[:, :].bitcast(F32)

        rBANK = [Res(f"bank{i}", excl=True) for i in range(8)]
        rWB = [Res(f"wb{i}") for i in range(4)]
        rHT = [Res(f"ht{k}") for k in range(16)]
        rHTH = Res("hth")
        rYT = [Res(f"yt{k}") for k in range(16)]
        rWOBF = Res("wobf")

        esem = {e: es.enter_context(nc.semaphore(f"s_{e}")) for e in ("pe", "act", "dve", "pool")}
        dsems = {e: [es.enter_context(nc.semaphore(f"d_{e}{i}")) for i in range(KDMA)] for e in ("sp", "pool")}

        def X(eng, meth, *a, rd=(), wr=(), ac=(), **kw):
            S.cur_lbl = meth + " " + " ".join(f"{k}={getattr(v, 'shape', v)}" for k, v in kw.items() if k in ("out", "in_", "in0", "lhsT", "rhs", "func"))
            return S.add(eng, lambda e: getattr(e, meth)(*a, **kw), reads=rd, writes=wr, accum=ac)

        def dma(q, out_ap, in_ap, rd=(), wr=(), ac=()):
            S.cur_lbl = f"dma out={out_ap.shape} in={in_ap.shape}"
            return S.add(q, lambda e: e.dma_start(out=out_ap, in_=in_ap), reads=rd, writes=wr, accum=ac, dma=True)

        def MM(o, lhsT, rhs, start, stop, rd, bankres, first):
            X("pe", "matmul", o, lhsT=lhsT, rhs=rhs, start=start, stop=stop, rd=rd,
              wr=[bankres] if first else (), ac=() if first else [bankres])

        wslot = [0]

        rSTART = Res("startup_done")
        nload = [0]

        def load_chunk(src, c0):
            s = wslot[0] % 4
            wslot[0] += 1
            src_ap = src.rearrange("(k p) n -> p k n", p=128)[:, :, c0:c0 + 256]
            extra = [rSTART] if 1 <= nload[0] <= 3 else []
            nload[0] += 1
            dma("pool", WB4[:, s, :, :], src_ap, rd=extra, wr=[rWB[s]])
            return s

        ident = carve(R, 8192, 64, BF16)
        rC = Res("consts")
        for (dst, src) in ((ccol[:, :], c_col), (bcol[:, :], b_col), (gcol[:, :], g_col), (gfcol[:, :], gf_col),
                           (lgcol[:, :], lg_col), (esk[:, :], sinks_b), (ident, ident_in)):
            dma("sp", dst, src[:, :], ac=[rC])
        X("dve", "memset", onesf[:, :], 1.0, ac=[rC])
        rMOD = Res("modcol")
        X("dve", "memset", modcol[:, :], 0.0, wr=[rMOD])
        rT = Res("tmpc")
        X("act", "activation", out=sm[:, 0:16], in_=ccol[:, :], func=AF.Tanh, scale=0.5, rd=[rC], wr=[rT])
        X("dve", "scalar_tensor_tensor", out=sm[:, 16:32], in0=sm[:, 0:16], scalar=1.0, in1=ccol[:, :],
          op0=ALU.add, op1=ALU.mult, rd=[rT, rC], wr=[rT])
        rCACT = Res("cact")
        X("dve", "tensor_scalar", out=cact[:, :], in0=sm[:, 16:32], scalar1=0.5, scalar2=None, op0=ALU.mult,
          rd=[rT], wr=[rCACT])
        rESK = Res("esk")
        X("act", "activation", out=esk[:, :], in_=esk[:, :], func=AF.Exp, rd=[rC], wr=[rESK])
        rLG = Res("lg")
        X("dve", "tensor_scalar", out=lgcol[:, :], in0=lgcol[:, :], scalar1=0.5, scalar2=None, op0=ALU.mult,
          rd=[rC], wr=[rLG])

        rowbuf = [rowt[:, 0:256], rowt[:, 256:512]]
        rROW = [Res("row0"), Res("row1")]
        gem_pending = []
        gem_count = [0]

        YS4 = YT[:, :].rearrange("p (s k n) -> p s k n", s=8, k=16)
        rYS = [Res(f"ys{i}") for i in range(8)]

        def gemv_chunk(ci, rowbank, colbank, startup=False):
            if len(gem_pending) >= 2:
                gem_flush(len(gem_pending) - 1)
            if startup:
                if ci < 8:
                    wt, rw = YS4[:, ci], rYS[ci]
                    src_ap = w_ada.rearrange("(k p) n -> p k n", p=128)[:, :, ci * 256:(ci + 1) * 256]
                    dma("pool", wt, src_ap, wr=[rw])
                elif ci < 12:
                    s = load_chunk(w_ada, ci * 256)
                    wt, rw = WB4[:, s], rWB[s]
                else:
                    wt, rw = YS4[:, ci - 12], rYS[ci - 12]
                    src_ap = w_ada.rearrange("(k p) n -> p k n", p=128)[:, :, ci * 256:(ci + 1) * 256]
                    dma("pool", wt, src_ap, wr=[rw])
            else:
                if ci < 24:
                    s = load_chunk(w_ada, ci * 256)
                else:
                    s = load_chunk(w_adaf, (ci - 24) * 256)
                wt, rw = WB4[:, s], rWB[s]
            rb = gem_count[0] % 2
            gem_count[0] += 1
            bk = bank(rowbank)
            for kt in range(16):
                MM(bk[0:1, 0:256], cact[:, kt:kt + 1], wt[:, kt, :], kt == 0, kt == 15,
                   [rw, rCACT], rBANK[rowbank], kt == 0)
            X("act", "activation", out=rowbuf[rb][0:1, :], in_=bk[0:1, 0:256], func=AF.Copy,
              rd=[rBANK[rowbank]], wr=[rROW[rb]])

            def tr():
                cb = bank(colbank)
                c0 = (ci % 8) * 2
                for j in range(2):
                    MM(cb[:, c0 + j:c0 + j + 1], rowbuf[rb][0:1, j * 128:(j + 1) * 128], onesf[0:1, 0:1], True, True,
                       [rROW[rb], rC], rBANK[colbank], j == 0)
                X("dve", "tensor_copy", out=modcol[:, 2 * ci:2 * ci + 2], in_=cb[:, c0:c0 + 2],
                  rd=[rBANK[colbank]], ac=[rMOD])
            gem_pending.append(tr)

        def gem_flush(n=None):
            k = 0
            while gem_pending and (n is None or k < n):
                gem_pending.pop(0)()
                k += 1

        xst = [carve(R, i * 2048, 2048) for i in range(3)]
        xnb = [carve(R, 6144 + i * 1024, 1024, BF16) for i in range(2)]
        rXST = [Res("xst0"), Res("xst1"), Res("xst2")]
        rXNB = [Res("xnb0"), Res("xnb1")]
        rSS = Res("ss")
        X("dve", "memset", sm[:, 32:96], 0.0, wr=[rSS])

        def xstage_a(blk):
            bi = blk % 2
            xi = blk % 3
            dma("sp", xst[xi], xs[blk * 128:(blk + 1) * 128, :], wr=[rXST[xi]])
            ssb = sm[:, 32 + blk:33 + blk]
            rs = sm[:, 49 + blk:50 + blk]
            X("act", "activation", out=xnb[bi], in_=xst[xi], func=AF.Square, accum_out=ssb,
              rd=[rXST[xi], rSS], wr=[rXNB[bi]], ac=[rSS])
            X("dve", "tensor_scalar", out=rs, in0=ssb, scalar1=1.0 / D, scalar2=EPS, op0=ALU.mult, op1=ALU.add,
              rd=[rSS], ac=[rSS])
            X("act", "activation", out=rs, in_=rs, func=AF.Sqrt, rd=[rSS], ac=[rSS])
            X("dve", "reciprocal", out=rs, in_=rs, rd=[rSS], ac=[rSS])
            X("dve", "tensor_scalar", out=xnb[bi], in0=xst[xi], scalar1=rs, scalar2=None, op0=ALU.mult,
              rd=[rXST[xi], rSS], wr=[rXNB[bi]])

        def xstage_b(blk):
            bi = blk % 2
            b0 = 2 * (blk % 2)
            for hb in range(2):
                bk = bankbf(b0 + hb)
                for j in range(8):
                    kt = hb * 8 + j
                    X("pe", "transpose", bk[:, j * 128:(j + 1) * 128], xnb[bi][:, kt * 128:(kt + 1) * 128], ident,
                      rd=[rXNB[bi], rC], wr=[rBANK[b0 + hb]] if j == 0 else (), ac=() if j == 0 else [rBANK[b0 + hb]])
                src = bk.rearrange("p (a b) -> p a b", a=8)
                if blk == 0:
                    dst = HTH3[:, hb * 8:(hb + 1) * 8, :]
                    acc = [rHTH]
                else:
                    dst = HT3[:, hb * 8:(hb + 1) * 8, (blk - 1) * 128:blk * 128]
                    acc = [rHT[k] for k in range(hb * 8, hb * 8 + 8)]
                if hb == 0:
                    X("act", "activation", out=dst, in_=src, func=AF.Copy, rd=[rBANK[b0 + hb]], ac=acc)
                else:
                    X("dve", "tensor_copy", out=dst, in_=src, rd=[rBANK[b0 + hb]], ac=acc)

        xstage_a(0)
        for blk in range(17):
            if blk + 1 < 17:
                xstage_a(blk + 1)
            if blk < 16:
                gemv_chunk(blk, 5 + (blk % 2), 7, startup=True)
            xstage_b(blk)
            if blk >= 1:
                gem_flush(1)
        gem_flush()
        rMODB = Res("modb")
        X("dve", "tensor_tensor", out=modcol[:, 0:32], in0=modcol[:, 0:32], in1=bcol[:, 0:32], op=ALU.add,
          rd=[rMOD, rC], wr=[rMODB])
        X("dve", "scalar_tensor_tensor", out=acol[:, :], in0=modcol[:, 16:32], scalar=1.0, in1=gcol[:, :],
          op0=ALU.add, op1=ALU.mult, rd=[rMODB, rC], wr=[rMODB, rSTART])
        S.barrier()
        for kt in range(16):
            X("dve", "tensor_scalar", out=HTH3[:, kt, :], in0=HTH3[:, kt, :], scalar1=acol[:, kt:kt + 1],
              scalar2=modcol[:, kt:kt + 1], op0=ALU.mult, op1=ALU.add, rd=[rMODB], wr=[rHTH])

        def affine_tg(tg):
            for kt in range(16):
                X("dve", "tensor_scalar", out=HT3[:, kt, tg * 512:(tg + 1) * 512], in0=HT3[:, kt, tg * 512:(tg + 1) * 512],
                  scalar1=acol[:, kt:kt + 1], scalar2=modcol[:, kt:kt + 1], op0=ALU.mult, op1=ALU.add,
                  rd=[rMODB, rHT[kt]], ac=[rHT[kt]])
        if os.environ.get("KVERB"):
            print("ops after startup", S.nadd)
        if STOP <= 1:
            S.muted = True

        KT = carve(R, 0, 1088, BF16)
        VAF = carve(R, 1088, 2176, BF16)
        VA = VAF.rearrange("p (b e c) -> p b e c", b=17, e=2)
        MSK = carve(R, 3264, 512, BF16).rearrange("p (w c) -> p w c", w=2)
        CSB = [[carve(R, 3776 + (2 * i + j) * 512, 512) for j in range(2)] for i in range(2)]
        QSB = [carve(R, 5824 + i * 256, 256, BF16) for i in range(2)]
        T1 = [carve(R, 6336 + i * 512, 512) for i in range(2)]
        T2 = [carve(R, 7360 + i * 512, 512) for i in range(2)]
        ES2 = carve(R, 8384, 512).rearrange("p (e c) -> p e c", e=2)
        pswap = carve(R, 8896, 64, BF16)
        identb = carve(R, 8960, 64, BF16)
        SEL = [carve(R, 9024 + i * 64, 64, BF16) for i in range(2)]
        ESR = [carve(R, 9152 + i * 128, 128, BF16) for i in range(2)]
        NQB = 3
        QT = [carve(YTLOW, i * 512, 512, BF16).rearrange("p (j t) -> p j t", j=2) for i in range(NQB)]
        GZ = [carve(YTLOW, 1536 + i * 1024, 1024).rearrange("p (j t) -> p j t", j=2) for i in range(NQB)]
        NPT = 6
        PT = [carve(YTLOW, 4608 + i * 256, 256, BF16) for i in range(NPT)]
        REC = [carve(YTLOW, 6144 + i * 256, 256) for i in range(2)]
        TT = [carve(YTLOW, 6656 + i * 256, 256) for i in range(2)]
        CSH = [carve(YTLOW, 7168 + i * 128, 128) for i in range(2)]
        SG = [carve(YTLOW, 7424, 512), carve(YTLOW, 7424, 512)]

        rKT = Res("kt")
        rVA = Res("va")
        rMSK = Res("msk")
        rCSB = [Res("csb0"), Res("csb1")]
        rQSB = [Res("qsb0"), Res("qsb1")]
        rT1 = [Res("t1a"), Res("t1b")]
        rT2 = [Res("t2a"), Res("t2b")]
        rES2 = Res("es2")
        rQT = [Res(f"qt{i}") for i in range(NQB)]
        rGZ = [Res(f"gz{i}") for i in range(NQB)]
        rPT = [Res(f"pt{i}") for i in range(NPT)]
        rSG = [Res("sg0")] * 2
        rREC = [Res("rec0"), Res("rec1")]
        rTT = [Res("tt0"), Res("tt1")]
        rCSH = Res("csh")
        rPSW = Res("pswap")

        dma("sp", MSK, masks_in[:, :].rearrange("p (w c) -> p w c", w=2), wr=[rMSK])
        dma("sp", pswap, pswap_in[:, :], wr=[rPSW])
        dma("sp", identb, ident_in[:, :], ac=[rPSW])
        rSEL = Res("sel")
        X("dve", "memset", SEL[0][0:1, 0:64], 0.0, wr=[rSEL])
        X("dve", "memset", SEL[0][0:1, 64:128], 1.0, ac=[rSEL])
        X("dve", "memset", SEL[1][0:1, 0:64], 1.0, ac=[rSEL])
        X("dve", "memset", SEL[1][0:1, 64:128], 0.0, ac=[rSEL])
        dma("sp", CSH[0], cos_in[:, 0:128], wr=[rCSH])
        dma("sp", CSH[1], sin_in[:, 0:128], ac=[rCSH])

        accb = [0]
        ropei = [0]
        SWAPB = 2
        PVB = 5

        def next_acc():
            b = accb[0] % 2
            accb[0] += 1
            return b

        def rope_unit(ab, n, cos_ap, sin_ap, rtab, dst_ap, dst_accum, swapbanks=(2, 2)):
            i = ropei[0] % 2
            ropei[0] += 1
            SWAPB = swapbanks[i]
            src = bank(ab)[:, 0:n]
            X("act", "activation", out=QSB[i][:, 0:n], in_=src, func=AF.Copy, rd=[rBANK[ab]], wr=[rQSB[i]])
            sw = bank(SWAPB)[:, 0:n]
            MM(sw, pswap, QSB[i][:, 0:n], True, True, [rQSB[i], rPSW], rBANK[SWAPB], True)
            X("dve", "tensor_tensor", out=T1[i][:, 0:n], in0=src, in1=cos_ap, op=ALU.mult,
              rd=[rBANK[ab]] + rtab, wr=[rT1[i]])
            X("dve", "tensor_tensor", out=T2[i][:, 0:n], in0=sw, in1=sin_ap, op=ALU.mult,
              rd=[rBANK[SWAPB]] + rtab, wr=[rT2[i]])
            X("dve", "tensor_tensor", out=dst_ap, in0=T1[i][:, 0:n], in1=T2[i][:, 0:n], op=ALU.add,
              rd=[rT1[i], rT2[i]], ac=dst_accum)

        def fm_matmuls(ab, s, c0, rhs_fn, rhs_res, kts=range(16)):
            for kt in kts:
                rhs = rhs_fn(kt)
                n = rhs.shape[-1]
                MM(bank(ab)[:, 0:n], WB4[:, s, kt, c0:c0 + 128], rhs, kt == 0, kt == 15,
                   [rWB[s], rhs_res[kt]], rBANK[ab], kt == 0)

        gem_next = [16]

        def gem_bg(n, rows=(6, 6), col=7):
            for i in range(n):
                if gem_next[0] < 40:
                    gemv_chunk(gem_next[0], rows[i % 2], col)
                    gem_next[0] += 1
                    if i % 2 == 1:
                        gem_flush()
            gem_flush()

        csbuf = [0]

        def load_cs(tg):
            cb = csbuf[0] % 2
            csbuf[0] += 1
            dma("sp", CSB[cb][0], cos_in[:, 128 + tg * 512:128 + (tg + 1) * 512], wr=[rCSB[cb]])
            dma("sp", CSB[cb][1], sin_in[:, 128 + tg * 512:128 + (tg + 1) * 512], ac=[rCSB[cb]])
            return cb

        LA = 4
        sgi = [0]
        for t in range(2):
            skv = load_chunk(w_in, (5 * t) * 256)
            X("dve", "memset", VAF, 1.0, wr=[rVA])
            ab = next_acc()
            fm_matmuls(ab, skv, 0, lambda kt: HTH3[:, kt, :], [rHTH] * 16)
            rope_unit(ab, 128, CSH[0], CSH[1], [rCSH], KT[:, 0:128], [rKT], (2, 3))
            for tg in range(4):
                if t == 0:
                    affine_tg(tg)
                cb = load_cs(tg)
                ab = next_acc()
                fm_matmuls(ab, skv, 0, lambda kt: HT3[:, kt, tg * 512:(tg + 1) * 512], rHT)
                rope_unit(ab, 512, CSB[cb][0], CSB[cb][1], [rCSB[cb]], KT[:, 128 + tg * 512:128 + (tg + 1) * 512], [rKT], (2, 3))
            for blk in range(17):
                ab = next_acc()
                for kt in range(16):
                    lhsT = HTH3[:, kt, :] if blk == 0 else HT3[:, kt, (blk - 1) * 128:blk * 128]
                    MM(bank(ab)[:, 0:128], lhsT, WB4[:, skv, kt, 128:256], kt == 0, kt == 15,
                       [rWB[skv], rHTH if blk == 0 else rHT[kt]], rBANK[ab], kt == 0)
                X("act", "activation", out=VA[:, blk, 0, 0:64], in_=bank(ab)[:, 0:64], func=AF.Copy, rd=[rBANK[ab]], ac=[rVA])
                X("dve", "tensor_copy", out=VA[:, blk, 1, 64:128], in_=bank(ab)[:, 64:128], rd=[rBANK[ab]], ac=[rVA])
            gem_bg(1, rows=(3, 4), col=5)
            for gp in range(2):
                sq = load_chunk(w_in, (5 * t + 1 + 2 * gp) * 256)
                sz = load_chunk(w_in, (5 * t + 2 + 2 * gp) * 256)
                for e_ in range(2):
                    for j in range(2):
                        h = 8 * t + 4 * e_ + 2 * gp + j
                        fst = (e_ == 0 and j == 0)
                        X("dve", "tensor_scalar", out=ESR[e_][0:1, j * 128:(j + 1) * 128], in0=onesf[0:1, :],
                          scalar1=esk[0:1, h:h + 1], scalar2=None, op0=ALU.mult,
                          rd=[rESK, rC], wr=[rES2] if fst else (), ac=() if fst else [rES2])

                def make_group(kind, j, tg, st):
                    qb = tg % NQB
                    box = {}
                    sw = sq if kind == "q" else sz

                    def h0():
                        if "cb" not in st:
                            st["cb"] = load_cs(tg)
                        box["ab"] = next_acc()
                        fm_matmuls(box["ab"], sw, j * 128, lambda kt: HT3[:, kt, tg * 512:(tg + 1) * 512], rHT, range(0, 8))

                    def h1():
                        ab = box["ab"]
                        cb = st["cb"]
                        fm_matmuls(ab, sw, j * 128, lambda kt: HT3[:, kt, tg * 512:(tg + 1) * 512], rHT, range(8, 16))
                        if kind == "q":
                            rope_unit(ab, 512, CSB[cb][0], CSB[cb][1], [rCSB[cb]], QT[qb][:, j, :], [rQT[qb]])
                        else:
                            si = sgi[0] % 2
                            sgi[0] += 1
                            X("act", "activation", out=SG[si], in_=bank(ab)[:, :], func=AF.Exp, scale=-1.0,
                              rd=[rBANK[ab]], wr=[rSG[si]])
                            X("act", "activation", out=SG[si], in_=SG[si], func=AF.Ln, bias=1.0, rd=[rSG[si]], wr=[rSG[si]])
                            X("act", "activation", out=SG[si], in_=SG[si], func=AF.Exp, scale=-1.0, rd=[rSG[si]], wr=[rSG[si]])
                            X("dve", "tensor_tensor", out=GZ[qb][:, j, :], in0=bank(ab)[:, :], in1=SG[si], op=ALU.mult,
                              rd=[rBANK[ab], rSG[si]], ac=[rGZ[qb]])
                    return [h0, h1]

                def make_G(tg):
                    st = {}
                    return (make_group("q", 0, tg, st) + make_group("z", 0, tg, st) +
                            make_group("q", 1, tg, st) + make_group("z", 1, tg, st))

                def S_unit(k):
                    tg, i = divmod(k, 8)
                    nb, e_ = divmod(i, 2)
                    n = 4 * tg + nb
                    qb = tg % NQB
                    scb = 3 + k % 3
                    pt = PT[k % NPT]
                    rpt = rPT[k % NPT]
                    psl = slice(e_ * 64, (e_ + 1) * 64)
                    mi = 1 if n == 0 else 0
                    MM(bank(scb)[:, :], identb, MSK[:, mi, :], True, False, [rMSK, rPSW], rBANK[scb], True)
                    for w in range(2):
                        k0 = (n + 1 - w) * 128
                        MM(bank(scb)[:, w * 256:(w + 1) * 256].rearrange("p (j q) -> p j q", j=2),
                           KT[psl, k0:k0 + 128], QT[qb][psl, :, nb * 128:(nb + 1) * 128], False, (w == 1),
                           [rKT, rQT[qb]], rBANK[scb], False)
                    X("act", "activation", out=pt, in_=bank(scb)[:, :], func=AF.Exp, scale=0.125, rd=[rBANK[scb]], wr=[rpt])

                def PV_unit(k):
                    tg, i = divmod(k, 8)
                    nb, e_ = divmod(i, 2)
                    n = 4 * tg + nb
                    qb = tg % NQB
                    pt = PT[k % NPT]
                    rpt = rPT[k % NPT]
                    u = k % 2
                    pvb = 6 + u
                    psl = slice(e_ * 64, (e_ + 1) * 64)
                    dsl = slice((1 - e_) * 64, (2 - e_) * 64)
                    MM(bank(pvb)[:, 0:256], VA[:, n, e_, :], pt[:, 256:512], True, False, [rVA, rpt], rBANK[pvb], True)
                    MM(bank(pvb)[:, 0:256], VA[:, n + 1, e_, :], pt[:, 0:256], False, True, [rVA, rpt], rBANK[pvb], False)
                    for j in range(2):
                        h = 8 * t + 4 * e_ + 2 * gp + j
                        X("act", "activation", out=REC[u][psl, j * 128:(j + 1) * 128], in_=bank(pvb)[dsl, j * 128:(j + 1) * 128],
                          func=AF.Ln, bias=esk[dsl, h:h + 1], rd=[rBANK[pvb], rESK],
                          wr=[rREC[u]] if j == 0 else (), ac=() if j == 0 else [rREC[u]])
                    X("act", "activation", out=REC[u][psl, :], in_=REC[u][psl, :], func=AF.Exp, scale=-1.0, rd=[rREC[u]], wr=[rREC[u]])
                    X("dve", "tensor_tensor", out=TT[u][psl, :], in0=bank(pvb)[psl, 0:256], in1=REC[u][psl, :],
                      op=ALU.mult, rd=[rBANK[pvb], rREC[u]], wr=[rTT[u]])
                    k0t = 8 + 4 * t + 2 * gp
                    X("dve", "tensor_tensor", out=YT3[psl, k0t:k0t + 2, n * 128:(n + 1) * 128],
                      in0=TT[u][psl, :].rearrange("p (j q) -> p j q", j=2),
                      in1=GZ[qb][psl, :, nb * 128:(nb + 1) * 128], op=ALU.mult,
                      rd=[rTT[u], rGZ[qb]], ac=[rYT[k0t], rYT[k0t + 1]])

                for f in make_G(0):
                    f()
                for tg in range(4):
                    nxt = make_G(tg + 1) if tg < 3 else []
                    for i in range(8):
                        k = 8 * tg + i
                        S_unit(k)
                        if k - LA >= 0:
                            PV_unit(k - LA)
                        for _ in range(2 if i == 0 else (1 if i < 7 else 0)):
                            if nxt:
                                nxt.pop(0)()
                for k in range(32 - LA, 32):
                    PV_unit(k)
                gem_bg(2, rows=(3, 4), col=5)
        gem_flush()
        S.barrier()
        if os.environ.get("KVERB"):
            print("ops after B", S.nadd)
        if STOP <= 2:
            S.muted = True

        VH = carve(R, 0, 8192, BF16).rearrange("p (b c) -> p b c", b=16)
        CP = carve(R, 8192, 1024).rearrange("p (g t) -> p g t", g=8)
        WSB = carve(R, 9216, 512, BF16).rearrange("p (g t) -> p g t", g=8)
        STT_ = carve(R, 9728, 192).rearrange("p (b h s) -> p b h s", b=16, h=2)
        STF = carve(R, 9728, 192)
        MV = carve(R, 9920, 32).rearrange("p (b s) -> p b s", b=16)
        RS = carve(R, 9952, 16)
        NMR = carve(R, 9968, 16)
        WSF = carve(YTLOW, 0, 1024).rearrange("p (g t) -> p g t", g=8)
        RSUM = carve(YTLOW, 1024, 1024)
        LBR = carve(YTLOW, 2048, 1024)
        BSR = carve(YTLOW, 3072, 1024)
        TRI = carve(YTLOW, 4096, 128)
        rVH = [Res(f"vh{b}") for b in range(16)]
        rCP = Res("cp")
        rWSB = Res("wsb")
        rWSF = Res("wsf")
        rST = Res("st")
        rRS = Res("rs")
        rRSUM = Res("rsum")
        rSET = Res("setA")

        dma("sp", WSF, wst_in[:, :].rearrange("p (g t) -> p g t", g=8), wr=[rWSF])
        dma("sp", TRI, tri_in[:, :], wr=[rSET])
        dma("sp", LBR[0:1, :], lb_row[:, :], ac=[rSET])
        dma("sp", BSR[0:1, :], bs_row[:, :], ac=[rSET])
        X("dve", "tensor_tensor", out=WSF, in0=WSF, in1=TRI.unsqueeze(1).broadcast_to([128, 8, 128]), op=ALU.mult,
          rd=[rSET], wr=[rWSF])
        X("dve", "tensor_copy", out=WSB, in_=WSF, rd=[rWSF], wr=[rWSB])
        for g in range(8):
            bb = 2 if g < 4 else 3
            oc = (g % 4) * 128
            MM(bank(bb)[0:1, oc:oc + 128], onesf[:, 0:1], WSF[:, g, :], True, True, [rWSF, rC], rBANK[bb], g % 4 == 0)
        X("act", "activation", out=RSUM[0:1, 0:512], in_=bank(2)[0:1, :], func=AF.Copy, rd=[rBANK[2]], wr=[rRSUM])
        X("act", "activation", out=RSUM[0:1, 512:1024], in_=bank(3)[0:1, :], func=AF.Copy, rd=[rBANK[3]], ac=[rRSUM])
        for g in range(8):
            bb = 2 if g < 4 else 3
            oc = (g % 4) * 128
            MM(bank(bb)[:, oc:oc + 128], LBR[0:1, g * 128:(g + 1) * 128], RSUM[0:1, g * 128:(g + 1) * 128], True, False,
               [rRSUM, rSET], rBANK[bb], g % 4 == 0)
            MM(bank(bb)[:, oc:oc + 128], onesf[0:1, :], BSR[0:1, g * 128:(g + 1) * 128], False, True,
               [rSET, rC], rBANK[bb], False)
        X("act", "activation", out=CP[:, 0:4, :], in_=bank(2)[:, :].rearrange("p (g t) -> p g t", g=4), func=AF.Identity,
          scale=0.5, rd=[rBANK[2]], wr=[rCP])
        X("act", "activation", out=CP[:, 4:8, :], in_=bank(3)[:, :].rearrange("p (g t) -> p g t", g=4), func=AF.Identity,
          scale=0.5, rd=[rBANK[3]], ac=[rCP])

        X("dve", "memset", STF, 0.0, wr=[rST])
        vslots = [load_chunk(w_in, (10 + i) * 256) for i in range(4)]
        for hh in range(2):
            s0, s1 = vslots[2 * hh], vslots[2 * hh + 1]
            for blk in range(16):
                ab = next_acc()
                assert s1 == s0 + 1
                for kt in range(16):
                    MM(bank(ab)[:, :].rearrange("p (a b) -> p a b", a=2), HT3[:, kt, blk * 128:(blk + 1) * 128],
                       WB4[:, s0:s0 + 2, kt, :], kt == 0, kt == 15, [rWB[s0], rWB[s1], rHT[kt]], rBANK[ab], kt == 0)
                X("dve", "bn_stats", out=STT_[:, blk, hh, :], in_=bank(ab)[:, :], rd=[rBANK[ab]], ac=[rST])
                X("act", "activation", out=VH[:, blk, hh * 512:(hh + 1) * 512], in_=bank(ab)[:, :], func=AF.Copy,
                  rd=[rBANK[ab]], ac=[rVH[blk]])
            if hh == 0:
                gem_bg(2)
        gem_flush()
        rMV = Res("mv")
        for blk in range(16):
            X("dve", "bn_aggr", out=MV[:, blk, :], in_=STT_[:, blk, :, :].rearrange("p h s -> p (h s)"),
              rd=[rST], wr=[rMV] if blk == 0 else (), ac=() if blk == 0 else [rMV])
        X("dve", "tensor_scalar", out=RS, in0=MV[:, :, 1], scalar1=EPS, scalar2=None, op0=ALU.add, rd=[rMV], wr=[rRS])
        X("act", "activation", out=RS, in_=RS, func=AF.Sqrt, rd=[rRS], wr=[rRS])
        X("dve", "reciprocal", out=RS, in_=RS, rd=[rRS], wr=[rRS])
        X("dve", "scalar_tensor_tensor", out=NMR, in0=MV[:, :, 0], scalar=-1.0, in1=RS, op0=ALU.mult, op1=ALU.mult,
          rd=[rRS, rMV], wr=[rRS])
        for blk in range(16):
            X("dve", "tensor_scalar", out=VH[:, blk, :], in0=VH[:, blk, :], scalar1=RS[:, blk:blk + 1],
              scalar2=NMR[:, blk:blk + 1], op0=ALU.mult, op1=ALU.add, rd=[rRS], wr=[rVH[blk]])
        if STOP <= 3:
            S.muted = True

        S1 = carve(HTHW, 0, 512)
        TH = carve(HTHW, 512, 512)
        rS1 = Res("s1")
        rTH = Res("th")
        unit = [0]
        for g in range(8):
            sg = load_chunk(w_in, (14 + g) * 256)
            if g % 2 == 0:
                qd = g // 2
                dma("pool", wo_bf[qd * 512:(qd + 1) * 512, :], w_out[qd * 512:(qd + 1) * 512, :],
                    wr=[rWOBF] if qd == 0 else (), ac=() if qd == 0 else [rWOBF])
            for tg in range(4):
                b0 = 3 * (unit[0] % 2)
                unit[0] += 1
                bU, bZ, bA = b0, b0 + 1, b0 + 2
                fm_matmuls(bU, sg, 0, lambda kt: HT3[:, kt, tg * 512:(tg + 1) * 512], rHT)
                fm_matmuls(bZ, sg, 128, lambda kt: HT3[:, kt, tg * 512:(tg + 1) * 512], rHT)
                for c in range(4):
                    blk = 4 * tg + c
                    MM(bank(bA)[:, c * 128:(c + 1) * 128], VH[:, blk, g * 128:(g + 1) * 128], WSB[:, g, :], True, True,
                       [rVH[blk], rWSB], rBANK[bA], c == 0)
                X("dve", "scalar_tensor_tensor", out=S1.rearrange("p (c t) -> p c t", c=4),
                  in0=bank(bA)[:, :].rearrange("p (c t) -> p c t", c=4), scalar=lgcol[:, g:g + 1],
                  in1=CP[:, g, :].unsqueeze(1).broadcast_to([128, 4, 128]), op0=ALU.mult, op1=ALU.add,
                  rd=[rBANK[bA], rCP, rLG], wr=[rS1])
                X("dve", "tensor_tensor", out=S1, in0=S1, in1=bank(bU)[:, :], op=ALU.mult, rd=[rBANK[bU]], wr=[rS1])
                X("act", "activation", out=TH, in_=bank(bZ)[:, :], func=AF.Tanh, scale=0.5, rd=[rBANK[bZ]], wr=[rTH])
                X("dve", "scalar_tensor_tensor", out=TH, in0=TH, scalar=1.0, in1=bank(bZ)[:, :], op0=ALU.add, op1=ALU.mult,
                  rd=[rBANK[bZ]], wr=[rTH])
                X("dve", "tensor_tensor", out=YT3[:, g, tg * 512:(tg + 1) * 512], in0=S1, in1=TH, op=ALU.mult,
                  rd=[rS1, rTH], ac=[rYT[g]])
                gem_flush()
            gem_bg(2 if g < 4 else 1)
        gem_flush()
        S.barrier()

        if STOP <= 4:
            S.muted = True
        if DBG:
            dma("sp", dbg_yt[:, :], YT[:, :], rd=rYT)
            dma("sp", dbg_ht[:, :], HT[:, :], rd=rHT)
            S.barrier()
        X("dve", "tensor_tensor", out=modcol[:, 32:80], in0=modcol[:, 32:80], in1=bcol[:, 32:80], op=ALU.add,
          rd=[rMOD, rC], wr=[rMODB])
        X("dve", "scalar_tensor_tensor", out=afcol[:, :], in0=modcol[:, 64:80], scalar=1.0, in1=gfcol[:, :],
          op0=ALU.add, op1=ALU.mult, rd=[rMODB, rC], wr=[rMODB])
        XR = [carve(R, i * 2048, 2048) for i in range(2)] + [carve(WBW, 6144, 2048)]
        OUTB = [carve(R, 4096 + i * 2048, 2048) for i in range(2)]
        identf = carve(R, 8192, 128)
        TMP = [carve(R, 8320 + i * 512, 512) for i in range(2)]
        GB = carve(WBW, 0, 2048)
        AFBT = carve(WBW, 2048, 2048)
        SFB = carve(WBW, 4096, 2048)
        DG = [carve(HTHW, i * 128, 128) for i in range(2)]
        WO3 = HT3
        rXR = [Res("xr0"), Res("xr1"), Res("xr2")]
        rOUTB = [Res("ob0"), Res("ob1")]
        rTMP = [Res("tmp0"), Res("tmp1")]
        rGB = Res("gb")
        rAFB = Res("afb")
        rSFB = Res("sfb")
        rDG = [Res("dg0"), Res("dg1")]
        rSS2 = Res("ss2")
        rIDF = Res("identf")
        dma("sp", identf, identf_in[:, :], wr=[rIDF])
        for kt in range(16):
            dma("sp", WO3[:, kt, :], wo_bf[kt * 128:(kt + 1) * 128, :], rd=[rWOBF], wr=[rHT[kt]])

        def bcast_build(colap, dst, rdst):
            for c in range(16):
                i = c % 2
                X("dve", "tensor_scalar", out=DG[i], in0=identf, scalar1=colap[:, c:c + 1], scalar2=None, op0=ALU.mult,
                  rd=[rMODB, rIDF], wr=[rDG[i]])
                bb = (c // 4) % 2
                MM(bank(bb)[:, (c % 4) * 128:(c % 4 + 1) * 128], onesf[:, :], DG[i], True, True, [rDG[i], rC], rBANK[bb], c % 4 == 0)
                if c % 4 == 3:
                    q4 = c // 4
                    X("act", "activation", out=dst[:, q4 * 512:(q4 + 1) * 512], in_=bank(bb)[:, :], func=AF.Copy,
                      rd=[rBANK[bb]], wr=[rdst] if q4 == 0 else (), ac=() if q4 == 0 else [rdst])

        bcast_build(modcol[:, 32:48], GB, rGB)
        bcast_build(afcol, AFBT, rAFB)
        bcast_build(modcol[:, 48:64], SFB, rSFB)
        X("dve", "memset", sm[:, 32:96], 0.0, wr=[rSS2])
        ti = [0]
        pending_final = []
        for blk in range(16):
            i = blk % 3
            o = blk % 2
            dma("sp", XR[i], xs[(blk + 1) * 128:(blk + 2) * 128, :], wr=[rXR[i]])
            b0 = 4 * (blk % 2)
            for cg in range(4):
                for kt in range(16):
                    MM(bank(b0 + cg)[:, :], YT3[:, kt, blk * 128:(blk + 1) * 128], WO3[:, kt, cg * 512:(cg + 1) * 512],
                       kt == 0, kt == 15, [rYT[kt], rHT[kt]], rBANK[b0 + cg], kt == 0)
                tb = ti[0] % 2
                ti[0] += 1
                X("dve", "tensor_tensor", out=TMP[tb], in0=bank(b0 + cg)[:, :], in1=GB[:, cg * 512:(cg + 1) * 512], op=ALU.mult,
                  rd=[rBANK[b0 + cg], rGB], wr=[rTMP[tb]])
                X("dve", "tensor_tensor", out=XR[i][:, cg * 512:(cg + 1) * 512], in0=TMP[tb],
                  in1=XR[i][:, cg * 512:(cg + 1) * 512], op=ALU.add, rd=[rTMP[tb], rXR[i]], ac=[rXR[i]])
            ssb = sm[:, 32 + blk:33 + blk]
            rsb = sm[:, 49 + blk:50 + blk]
            if pending_final:
                pending_final.pop(0)()
            X("act", "activation", out=OUTB[o], in_=XR[i], func=AF.Square, accum_out=ssb,
              rd=[rXR[i], rSS2], wr=[rOUTB[o]], ac=[rSS2])

            def final_ops(blk=blk, i=i, o=o, ssb=ssb, rsb=rsb):
                X("dve", "tensor_scalar", out=rsb, in0=ssb, scalar1=1.0 / D, scalar2=EPS, op0=ALU.mult, op1=ALU.add,
                  rd=[rSS2], ac=[rSS2])
                X("act", "activation", out=rsb, in_=rsb, func=AF.Sqrt, rd=[rSS2], ac=[rSS2])
                X("dve", "reciprocal", out=rsb, in_=rsb, rd=[rSS2], ac=[rSS2])
                X("dve", "scalar_tensor_tensor", out=OUTB[o], in0=XR[i], scalar=rsb, in1=AFBT, op0=ALU.mult, op1=ALU.mult,
                  rd=[rXR[i], rSS2, rAFB], wr=[rOUTB[o]])
                X("dve", "tensor_tensor", out=OUTB[o], in0=OUTB[o], in1=SFB, op=ALU.add, rd=[rSFB], wr=[rOUTB[o]])
                dma("pool", out[blk * 128:(blk + 1) * 128, :], OUTB[o], rd=[rOUTB[o]])
            pending_final.append(final_ops)
        while pending_final:
            pending_final.pop(0)()
        if DBG:
            dma("sp", dbg_mod[:, :], modcol[:, :], rd=[rMODB])
        S.muted = False
        S.force = True
        S.barrier()
        S.add("sp", lambda e: e.nop())

        if os.environ.get("KTRACE"):
            lo, hi = [int(v) for v in os.environ["KTRACE"].split(",")]
            for (i, e, l) in S.trace_lbl:
                if lo <= i <= hi:
                    print(i, e, l)
        S.finalize()
        block = es.enter_context(nc.Block())

        @block.tensor
        def _(e):
            S.emit("pe", e, esem, dsems)

        @block.scalar
        def _(e):
            S.emit("act", e, esem, dsems)

        @block.vector
        def _(e):
            S.emit("dve", e, esem, dsems)

        @block.gpsimd
        def _(e):
            S.emit("pool", e, esem, dsems)

        @block.sync
        def _(e):
            S.emit("sp", e, esem, dsems)
    return nc


def _col(v, n):
    return np.ascontiguousarray(np.asarray(v, np.float32).reshape(n, 128).T)


def _prep_shared(w_ada, b_ada, norm_g, w_in, ln_v_g, ln_v_b, w_spatial, b_spatial, sinks, w_out,
                 w_ada_final, b_ada_final, final_norm_g):
    f32 = np.float32
    w_in0 = np.asarray(w_in[0], f32)
    cols = []
    for t in range(2):
        kv = []
        for m in (2 * t, 2 * t + 1):
            kv.append(np.arange(4096 + m * 64, 4096 + (m + 1) * 64))
        for m in (2 * t, 2 * t + 1):
            kv.append(np.arange(4352 + m * 64, 4352 + (m + 1) * 64))
        cols.append(np.concatenate(kv))
        for gp in range(2):
            for base in (3072, 4608):
                cc = []
                for j in range(2):
                    g = 2 * gp + j
                    for h in (8 * t + g, 8 * t + 4 + g):
                        cc.append(np.arange(base + h * 64, base + (h + 1) * 64))
                cols.append(np.concatenate(cc))
    cols.append(np.arange(1024, 2048))
    for g in range(8):
        cols.append(np.arange(g * 128, (g + 1) * 128))
        cols.append(np.arange(2048 + g * 128, 2048 + (g + 1) * 128))
    perm = np.concatenate(cols)
    assert perm.shape[0] == 5632 and np.unique(perm).shape[0] == 5632
    w_in_p = np.ascontiguousarray(w_in0[:, perm])
    w_out0 = np.asarray(w_out[0], f32)
    rows = [np.arange(0, 1024)]
    for t in range(2):
        for g in range(4):
            for h in (8 * t + g, 8 * t + 4 + g):
                rows.append(np.arange(1024 + h * 64, 1024 + (h + 1) * 64))
    rperm = np.concatenate(rows)
    assert rperm.shape[0] == 2048 and np.unique(rperm).shape[0] == 2048
    w_out_p = np.ascontiguousarray(w_out0[rperm, :])
    b_col = np.concatenate([_col(b_ada[0], 48), _col(b_ada_final, 32)], axis=1)
    wst = np.ascontiguousarray(np.transpose(np.asarray(w_spatial[0], f32), (2, 0, 1)).reshape(128, 8 * 128))
    tri = (np.arange(128)[:, None] <= np.arange(128)[None, :]).astype(f32)
    ident = np.eye(128, dtype=f32)
    pswap = np.zeros((128, 128), f32)
    for base in (0, 64):
        for m in range(32):
            pswap[base + m + 32, base + m] = -1.0
            pswap[base + m, base + m + 32] = 1.0
    shared = {
        "w_ada": np.ascontiguousarray(np.asarray(w_ada[0], f32)),
        "w_adaf": np.ascontiguousarray(np.asarray(w_ada_final, f32)),
        "b_col": np.ascontiguousarray(b_col),
        "g_col": _col(norm_g[0], 16),
        "gf_col": _col(final_norm_g, 16),
        "w_in": w_in_p,
        "w_out": w_out_p,
        "lg_col": _col(ln_v_g[0], 8),
        "lb_row": np.ascontiguousarray(np.asarray(ln_v_b[0], f32).reshape(1, 1024)),
        "wst": wst,
        "bs_row": np.ascontiguousarray(np.asarray(b_spatial[0], f32).reshape(1, 1024)),
        "tri": tri,
        "sinks_b": np.ascontiguousarray(np.broadcast_to(np.asarray(sinks[0], f32).reshape(1, 16), (128, 16))),
        "ident": ident.astype(ml_dtypes.bfloat16),
        "identf": ident,
        "pswap": pswap.astype(ml_dtypes.bfloat16),
    }
    return shared


def _rope_tables(half):
    f32 = np.float32
    pos_i = np.arange(17 * 128, dtype=np.int64) + half * 2048 - 128
    cos = sin = None
    try:
        import jax
        import jax.numpy as jnp
        cpu = jax.devices("cpu")[0]
        with jax.default_device(cpu):
            inv_freq_j = 10000.0 ** (-jnp.arange(0, 64, 2, dtype=jnp.float32) / 64)
            ang_j = jnp.asarray(pos_i.astype(f32))[:, None] * inv_freq_j[None, :]
            cos = np.asarray(jnp.cos(ang_j), dtype=f32).T
            sin = np.asarray(jnp.sin(ang_j), dtype=f32).T
    except Exception:
        cos = sin = None
    if cos is None:
        inv_freq = (np.float32(10000.0) ** (-(np.arange(0, 64, 2, dtype=f32)) / np.float32(64))).astype(f32)
        ang = (pos_i.astype(f32)[:, None] * inv_freq[None, :]).astype(f32)
        cos = np.cos(ang).astype(f32).T
        sin = np.sin(ang).astype(f32).T
    cosT = np.ascontiguousarray(np.tile(cos, (4, 1)))
    sinT = np.ascontiguousarray(np.tile(sin, (4, 1)))
    return cosT, sinT


def _masks(half):
    j = np.arange(128)[:, None]
    q = np.arange(128)[None, :]
    cur = (j <= q).astype(np.float32)
    prev = (j > q).astype(np.float32)
    pf = prev if half == 1 else np.zeros_like(prev)
    m0 = np.concatenate([cur, cur, prev, prev], axis=1)
    m1 = np.concatenate([cur, cur, pf, pf], axis=1)
    valid = np.concatenate([m0, m1], axis=1)
    bias = np.where(valid > 0.5, np.float32(0.0), np.float32(-30000.0)).astype(np.float32)
    return np.ascontiguousarray(bias).astype(ml_dtypes.bfloat16)


_NC_CACHE = {}


def kernel(x, c, w_ada, b_ada, norm_g, w_in, ln_v_g, ln_v_b, w_spatial, b_spatial, sinks, w_out,
           w_ada_final, b_ada_final, final_norm_g):
    x = np.asarray(x, np.float32)
    c = np.asarray(c, np.float32)
    shared = _prep_shared(w_ada, b_ada, norm_g, w_in, ln_v_g, ln_v_b, w_spatial, b_spatial, sinks, w_out,
                          w_ada_final, b_ada_final, final_norm_g)
    in_maps = []
    for core in range(NCORES):
        b, half = core // 2, core % 2
        own = x[b, half * 2048:(half + 1) * 2048]
        halo = x[b, 1920:2048] if half == 1 else np.zeros((128, D), np.float32)
        cosT, sinT = _rope_tables(half)
        m = dict(shared)
        m["xs"] = np.ascontiguousarray(np.concatenate([halo, own], axis=0))
        m["c_col"] = _col(c[b], 16)
        m["masks"] = _masks(half)
        m["cosT"] = cosT
        m["sinT"] = sinT
        in_maps.append(m)
    if "nc" not in _NC_CACHE:
        _NC_CACHE["nc"] = build_nc()
    nc = _NC_CACHE["nc"]
    res = run_bass_kernel_spmd(nc, in_maps, core_ids=list(range(NCORES)))
    outp = np.empty((4, 4096, D), np.float32)
    for core in range(NCORES):
        b, half = core // 2, core % 2
        outp[b, half * 2048:(half + 1) * 2048] = np.asarray(res.results[core]["out"], np.float32)
    return outp
```

```python
import os
import numpy as np
import ml_dtypes
from contextlib import ExitStack
import concourse.bass as bass
import concourse.mybir as mybir
from concourse.bass_utils import run_bass_kernel_spmd

F32 = mybir.dt.float32
BF16 = mybir.dt.bfloat16
AF = mybir.ActivationFunctionType
ALU = mybir.AluOpType

NCORES = 8
D = 2048
TOK = 2048
NB = 16
EPS = 1e-5
NCH_IN = 22
KDMA = 8


class Res:
    __slots__ = ("name", "writers", "readers", "excl")

    def __init__(self, name, excl=False):
        self.name = name
        self.writers = []
        self.readers = []
        self.excl = excl


class Op:
    __slots__ = ("eng", "fn", "deps", "sig", "dma", "dma_idx", "cnt")


class Sched:
    ENGS = ("pe", "act", "dve", "pool", "sp")

    def __init__(self):
        self.ops = {e: [] for e in self.ENGS}
        self.ndma = {e: 0 for e in self.ENGS}
        self.dma_ops = {e: [] for e in self.ENGS}
        self.floor = {}
        self.muted = False
        self.nadd = 0
        self.maxops = int(os.environ.get("KMAXOPS", "100000000"))

    def add(self, eng, fn, reads=(), writes=(), accum=(), dma=False):
        if self.muted:
            return None
        self.nadd += 1
        if getattr(self, "trace_lbl", None) is not None:
            self.trace_lbl.append((self.nadd, eng, getattr(self, "cur_lbl", "?")))
        if self.nadd > self.maxops and not getattr(self, "force", False):
            return None
        lst = self.ops[eng]
        idx = len(lst)
        deps = dict(self.floor.get(eng, {}))
        if eng in self.floor:
            del self.floor[eng]

        def need(h):
            e, i = h
            if e == eng and i >= idx:
                return
            if e == "pe" and eng == "pe":
                return
            key = (e, i) if self.ops[e][i].dma else (e, None)
            if key[1] is None:
                deps[key] = max(deps.get(key, -1), i)
            else:
                deps[key] = i

        for r in reads:
            for h in r.writers:
                need(h)
            if r.excl:
                for h in r.readers:
                    if h[0] != eng:
                        need(h)
        for r in writes:
            for h in r.writers:
                need(h)
            for h in r.readers:
                need(h)
        for r in accum:
            for h in r.readers:
                need(h)
            if r.writers:
                need(r.writers[0])
        op = Op()
        op.eng = eng
        op.fn = fn
        op.deps = [(k[0], v) for k, v in deps.items()]
        op.sig = False
        op.dma = dma
        op.dma_idx = None
        op.cnt = None
        if dma:
            op.dma_idx = self.ndma[eng]
            self.ndma[eng] += 1
            self.dma_ops[eng].append(idx)
            j = op.dma_idx
            if j >= KDMA:
                op.deps.append((eng, self.dma_ops[eng][j - KDMA]))
        lst.append(op)
        h = (eng, idx)
        for r in reads:
            r.readers.append(h)
        for r in writes:
            r.writers = [h]
            r.readers = []
        for r in accum:
            r.writers.append(h)
        return h

    def barrier(self):
        fl = {}
        for e in self.ENGS:
            n = len(self.ops[e])
            if n == 0:
                continue
            for i in range(n - 1, -1, -1):
                if not self.ops[e][i].dma:
                    fl[(e, None)] = i
                    break
            for i in self.dma_ops[e][-KDMA:]:
                fl[(e, i)] = i
        for e in self.ENGS:
            if e == "pool":
                continue
            self.floor[e] = dict(fl)

    def finalize(self):
        for e in self.ENGS:
            for op in self.ops[e]:
                for (de, di) in op.deps:
                    t = self.ops[de][di]
                    if not t.dma:
                        t.sig = True
        for e in self.ENGS:
            c = 0
            for op in self.ops[e]:
                if (not op.dma) and op.sig:
                    c += 1
                op.cnt = c

    def emit(self, ename, engobj, esem, dsems):
        waited = {}
        for op in self.ops[ename]:
            for (de, di) in op.deps:
                t = self.ops[de][di]
                if t.dma:
                    sem = dsems[de][t.dma_idx % KDMA]
                    val = 16 * (t.dma_idx // KDMA + 1)
                else:
                    sem = esem[de]
                    val = t.cnt
                key = id(sem)
                if waited.get(key, 0) < val:
                    engobj.wait_ge(sem, val)
                    waited[key] = val
            inst = op.fn(engobj)
            if op.dma:
                inst.then_inc(dsems[ename][op.dma_idx % KDMA], 16)
            elif op.sig:
                inst.then_inc(esem[ename], 1)
        return waited


def build_nc():
    nc = bass.Bass("TRN2", target_bir_lowering=False)

    def din(name, shape, dt=F32):
        return nc.dram_tensor(name, list(shape), dt, kind="ExternalInput").ap()

    xs = din("xs", [17 * 128, D])
    c_col = din("c_col", [128, 16])
    w_ada = din("w_ada", [D, 6144])
    w_adaf = din("w_adaf", [D, 4096])
    b_col = din("b_col", [128, 80])
    g_col = din("g_col", [128, 16])
    gf_col = din("gf_col", [128, 16])
    w_in = din("w_in", [D, 5632])
    w_out = din("w_out", [D, D])
    lg_col = din("lg_col", [128, 8])
    lb_row = din("lb_row", [1, 1024])
    wst_in = din("wst", [128, 8 * 128])
    bs_row = din("bs_row", [1, 1024])
    tri_in = din("tri", [128, 128])
    sinks_b = din("sinks_b", [128, 16])
    ident_in = din("ident", [128, 128], BF16)
    identf_in = din("identf", [128, 128])
    pswap_in = din("pswap", [128, 128], BF16)
    masks_in = din("masks", [128, 2 * 512], BF16)
    cos_in = din("cosT", [128, 17 * 128])
    sin_in = din("sinT", [128, 17 * 128])
    out = nc.dram_tensor("out", [TOK, D], F32, kind="ExternalOutput").ap()
    wo_bf = nc.dram_tensor("wo_bf", [D, D], BF16, kind="Internal").ap()
    DBG = os.environ.get("KDBG") == "1"
    STOP = int(os.environ.get("KSTOP", "99"))
    if DBG:
        dbg_mod = nc.dram_tensor("dbg_mod", [128, 80], F32, kind="ExternalOutput").ap()
        dbg_yt = nc.dram_tensor("dbg_yt", [128, 16 * 2048], BF16, kind="ExternalOutput").ap()
        dbg_ht = nc.dram_tensor("dbg_ht", [128, 16 * 2048], BF16, kind="ExternalOutput").ap()

    S = Sched()
    if os.environ.get("KTRACE"):
        S.trace_lbl = []
    es = ExitStack()
    with es:
        es.enter_context(nc.allow_low_precision("bf16 matmul operands, fp32 accumulation"))

        def sb(name, shape, dt):
            return es.enter_context(nc.sbuf_tensor(name, list(shape), dt))

        HT = sb("HT", [128, 16 * 2048], BF16)
        HTH = sb("HTH", [128, 16 * 128], BF16)
        YT = sb("YT", [128, 16 * 2048], BF16)
        WB = sb("WB", [128, 4 * 16 * 256], BF16)
        R = sb("R", [128, 9984], F32)
        onesf = sb("onesf", [128, 128], F32)
        rowt = sb("rowt", [128, 512], F32)
        ccol = sb("ccol", [128, 16], F32)
        cact = sb("cact", [128, 16], BF16)
        modcol = sb("modcol", [128, 80], F32)
        bcol = sb("bcol", [128, 80], F32)
        gcol = sb("gcol", [128, 16], F32)
        gfcol = sb("gfcol", [128, 16], F32)
        acol = sb("acol", [128, 16], F32)
        afcol = sb("afcol", [128, 16], F32)
        lgcol = sb("lgcol", [128, 8], F32)
        esk = sb("esk", [128, 16], F32)
        sm = sb("sm", [128, 96], F32)
        PS = es.enter_context(nc.psum_tensor("PS", [128, 4096], F32))

        HT3 = HT[:, :].rearrange("p (k t) -> p k t", k=16)
        HTH3 = HTH[:, :].rearrange("p (k t) -> p k t", k=16)
        YT3 = YT[:, :].rearrange("p (k t) -> p k t", k=16)
        WB4 = WB[:, :].rearrange("p (s k n) -> p s k n", s=4, k=16)

        def bank(b):
            return PS[:, b * 512:(b + 1) * 512]

        def bankbf(b):
            return PS[:, b * 512:(b + 1) * 512].bitcast(BF16)

        def carve(region, off, words, dt=F32):
            ap = region[:, off:off + words]
            if dt is BF16:
                ap = ap.bitcast(BF16)
            return ap

        YTLOW = YT[:, 0:8 * 2048].bitcast(F32)
        HTHW = HTH[:, :].bitcast(F32)
        WBW = WB[:, :].bitcast(F32)

        rBANK = [Res(f"bank{i}", excl=True) for i in range(8)]
        rWB = [Res(f"wb{i}") for i in range(4)]
        rHT = [Res(f"ht{k}") for k in range(16)]
        rHTH = Res("hth")
        rYT = [Res(f"yt{k}") for k in range(16)]
        rWOBF = Res("wobf")

        esem = {e: es.enter_context(nc.semaphore(f"s_{e}")) for e in ("pe", "act", "dve", "pool")}
        dsems = {e: [es.enter_context(nc.semaphore(f"d_{e}{i}")) for i in range(KDMA)] for e in ("sp", "pool")}

        def X(eng, meth, *a, rd=(), wr=(), ac=(), **kw):
            S.cur_lbl = meth + " " + " ".join(f"{k}={getattr(v, 'shape', v)}" for k, v in kw.items() if k in ("out", "in_", "in0", "lhsT", "rhs", "func"))
            return S.add(eng, lambda e: getattr(e, meth)(*a, **kw), reads=rd, writes=wr, accum=ac)

        def dma(q, out_ap, in_ap, rd=(), wr=(), ac=()):
            S.cur_lbl = f"dma out={out_ap.shape} in={in_ap.shape}"
            return S.add(q, lambda e: e.dma_start(out=out_ap, in_=in_ap), reads=rd, writes=wr, accum=ac, dma=True)

        def MM(o, lhsT, rhs, start, stop, rd, bankres, first):
            X("pe", "matmul", o, lhsT=lhsT, rhs=rhs, start=start, stop=stop, rd=rd,
              wr=[bankres] if first else (), ac=() if first else [bankres])

        wslot = [0]

        rSTART = Res("startup_done")
        nload = [0]

        def load_chunk(src, c0):
            s = wslot[0] % 4
            wslot[0] += 1
            src_ap = src.rearrange("(k p) n -> p k n", p=128)[:, :, c0:c0 + 256]
            extra = [rSTART] if 1 <= nload[0] <= 3 else []
            nload[0] += 1
            dma("pool", WB4[:, s, :, :], src_ap, rd=extra, wr=[rWB[s]])
            return s

        ident = carve(R, 8192, 64, BF16)
        rC = Res("consts")
        for (dst, src) in ((ccol[:, :], c_col), (bcol[:, :], b_col), (gcol[:, :], g_col), (gfcol[:, :], gf_col),
                           (lgcol[:, :], lg_col), (esk[:, :], sinks_b), (ident, ident_in)):
            dma("sp", dst, src[:, :], ac=[rC])
        X("dve", "memset", onesf[:, :], 1.0, ac=[rC])
        rMOD = Res("modcol")
        X("dve", "memset", modcol[:, :], 0.0, wr=[rMOD])
        rT = Res("tmpc")
        X("act", "activation", out=sm[:, 0:16], in_=ccol[:, :], func=AF.Tanh, scale=0.5, rd=[rC], wr=[rT])
        X("dve", "scalar_tensor_tensor", out=sm[:, 16:32], in0=sm[:, 0:16], scalar=1.0, in1=ccol[:, :],
          op0=ALU.add, op1=ALU.mult, rd=[rT, rC], wr=[rT])
        rCACT = Res("cact")
        X("dve", "tensor_scalar", out=cact[:, :], in0=sm[:, 16:32], scalar1=0.5, scalar2=None, op0=ALU.mult,
          rd=[rT], wr=[rCACT])
        rESK = Res("esk")
        X("act", "activation", out=esk[:, :], in_=esk[:, :], func=AF.Exp, rd=[rC], wr=[rESK])
        rLG = Res("lg")
        X("dve", "tensor_scalar", out=lgcol[:, :], in0=lgcol[:, :], scalar1=0.5, scalar2=None, op0=ALU.mult,
          rd=[rC], wr=[rLG])

        rowbuf = [rowt[:, 0:256], rowt[:, 256:512]]
        rROW = [Res("row0"), Res("row1")]
        gem_pending = []
        gem_count = [0]

        YS4 = YT[:, :].rearrange("p (s k n) -> p s k n", s=8, k=16)
        rYS = [Res(f"ys{i}") for i in range(8)]

        def gemv_chunk(ci, rowbank, colbank, startup=False):
            if len(gem_pending) >= 2:
                gem_flush(len(gem_pending) - 1)
            if startup:
                if ci < 8:
                    wt, rw = YS4[:, ci], rYS[ci]
                    src_ap = w_ada.rearrange("(k p) n -> p k n", p=128)[:, :, ci * 256:(ci + 1) * 256]
                    dma("pool", wt, src_ap, wr=[rw])
                elif ci < 12:
                    s = load_chunk(w_ada, ci * 256)
                    wt, rw = WB4[:, s], rWB[s]
                else:
                    wt, rw = YS4[:, ci - 12], rYS[ci - 12]
                    src_ap = w_ada.rearrange("(k p) n -> p k n", p=128)[:, :, ci * 256:(ci + 1) * 256]
                    dma("pool", wt, src_ap, wr=[rw])
            else:
                if ci < 24:
                    s = load_chunk(w_ada, ci * 256)
                else:
                    s = load_chunk(w_adaf, (ci - 24) * 256)
                wt, rw = WB4[:, s], rWB[s]
            rb = gem_count[0] % 2
            gem_count[0] += 1
            bk = bank(rowbank)
            for kt in range(16):
                MM(bk[0:1, 0:256], cact[:, kt:kt + 1], wt[:, kt, :], kt == 0, kt == 15,
                   [rw, rCACT], rBANK[rowbank], kt == 0)
            X("act", "activation", out=rowbuf[rb][0:1, :], in_=bk[0:1, 0:256], func=AF.Copy,
              rd=[rBANK[rowbank]], wr=[rROW[rb]])

            def tr():
                cb = bank(colbank)
                c0 = (ci % 8) * 2
                for j in range(2):
                    MM(cb[:, c0 + j:c0 + j + 1], rowbuf[rb][0:1, j * 128:(j + 1) * 128], onesf[0:1, 0:1], True, True,
                       [rROW[rb], rC], rBANK[colbank], j == 0)
                X("dve", "tensor_copy", out=modcol[:, 2 * ci:2 * ci + 2], in_=cb[:, c0:c0 + 2],
                  rd=[rBANK[colbank]], ac=[rMOD])
            gem_pending.append(tr)

        def gem_flush(n=None):
            k = 0
            while gem_pending and (n is None or k < n):
                gem_pending.pop(0)()
                k += 1

        xst = [carve(R, i * 2048, 2048) for i in range(3)]
        xnb = [carve(R, 6144 + i * 1024, 1024, BF16) for i in range(2)]
        rXST = [Res("xst0"), Res("xst1"), Res("xst2")]
        rXNB = [Res("xnb0"), Res("xnb1")]
        rSS = Res("ss")
        X("dve", "memset", sm[:, 32:96], 0.0, wr=[rSS])

        def xstage_a(blk):
            bi = blk % 2
            xi = blk % 3
            dma("sp", xst[xi], xs[blk * 128:(blk + 1) * 128, :], wr=[rXST[xi]])
            ssb = sm[:, 32 + blk:33 + blk]
            rs = sm[:, 49 + blk:50 + blk]
            X("act", "activation", out=xnb[bi], in_=xst[xi], func=AF.Square, accum_out=ssb,
              rd=[rXST[xi], rSS], wr=[rXNB[bi]], ac=[rSS])
            X("dve", "tensor_scalar", out=rs, in0=ssb, scalar1=1.0 / D, scalar2=EPS, op0=ALU.mult, op1=ALU.add,
              rd=[rSS], ac=[rSS])
            X("act", "activation", out=rs, in_=rs, func=AF.Sqrt, rd=[rSS], ac=[rSS])
            X("dve", "reciprocal", out=rs, in_=rs, rd=[rSS], ac=[rSS])
            X("dve", "tensor_scalar", out=xnb[bi], in0=xst[xi], scalar1=rs, scalar2=None, op0=ALU.mult,
              rd=[rXST[xi], rSS], wr=[rXNB[bi]])

        def xstage_b(blk):
            bi = blk % 2
            b0 = 2 * (blk % 2)
            for hb in range(2):
                bk = bankbf(b0 + hb)
                for j in range(8):
                    kt = hb * 8 + j
                    X("pe", "transpose", bk[:, j * 128:(j + 1) * 128], xnb[bi][:, kt * 128:(kt + 1) * 128], ident,
                      rd=[rXNB[bi], rC], wr=[rBANK[b0 + hb]] if j == 0 else (), ac=() if j == 0 else [rBANK[b0 + hb]])
                src = bk.rearrange("p (a b) -> p a b", a=8)
                if blk == 0:
                    dst = HTH3[:, hb * 8:(hb + 1) * 8, :]
                    acc = [rHTH]
                else:
                    dst = HT3[:, hb * 8:(hb + 1) * 8, (blk - 1) * 128:blk * 128]
                    acc = [rHT[k] for k in range(hb * 8, hb * 8 + 8)]
                if hb == 0:
                    X("act", "activation", out=dst, in_=src, func=AF.Copy, rd=[rBANK[b0 + hb]], ac=acc)
                else:
                    X("dve", "tensor_copy", out=dst, in_=src, rd=[rBANK[b0 + hb]], ac=acc)

        xstage_a(0)
        for blk in range(17):
            if blk + 1 < 17:
                xstage_a(blk + 1)
            if blk < 16:
                gemv_chunk(blk, 5 + (blk % 2), 7, startup=True)
            xstage_b(blk)
            if blk >= 1:
                gem_flush(1)
        gem_flush()
        rMODB = Res("modb")
        X("dve", "tensor_tensor", out=modcol[:, 0:32], in0=modcol[:, 0:32], in1=bcol[:, 0:32], op=ALU.add,
          rd=[rMOD, rC], wr=[rMODB])
        X("dve", "scalar_tensor_tensor", out=acol[:, :], in0=modcol[:, 16:32], scalar=1.0, in1=gcol[:, :],
          op0=ALU.add, op1=ALU.mult, rd=[rMODB, rC], wr=[rMODB, rSTART])
        S.barrier()
        for kt in range(16):
            X("dve", "tensor_scalar", out=HTH3[:, kt, :], in0=HTH3[:, kt, :], scalar1=acol[:, kt:kt + 1],
              scalar2=modcol[:, kt:kt + 1], op0=ALU.mult, op1=ALU.add, rd=[rMODB], wr=[rHTH])

        def affine_tg(tg):
            for kt in range(16):
                X("dve", "tensor_scalar", out=HT3[:, kt, tg * 512:(tg + 1) * 512], in0=HT3[:, kt, tg * 512:(tg + 1) * 512],
                  scalar1=acol[:, kt:kt + 1], scalar2=modcol[:, kt:kt + 1], op0=ALU.mult, op1=ALU.add,
                  rd=[rMODB, rHT[kt]], ac=[rHT[kt]])
        if os.environ.get("KVERB"):
            print("ops after startup", S.nadd)
        if STOP <= 1:
            S.muted = True

        KT = carve(R, 0, 1088, BF16)
        VAF = carve(R, 1088, 2176, BF16)
        VA = VAF.rearrange("p (b e c) -> p b e c", b=17, e=2)
        MSK = carve(R, 3264, 512, BF16).rearrange("p (w c) -> p w c", w=2)
        CSB = [[carve(R, 3776 + (2 * i + j) * 512, 512) for j in range(2)] for i in range(2)]
        QSB = [carve(R, 5824 + i * 256, 256, BF16) for i in range(2)]
        T1 = [carve(R, 6336 + i * 512, 512) for i in range(2)]
        T2 = [carve(R, 7360 + i * 512, 512) for i in range(2)]
        ES2 = carve(R, 8384, 512).rearrange("p (e c) -> p e c", e=2)
        pswap = carve(R, 8896, 64, BF16)
        identb = carve(R, 8960, 64, BF16)
        SEL = [carve(R, 9024 + i * 64, 64, BF16) for i in range(2)]
        ESR = [carve(R, 9152 + i * 128, 128, BF16) for i in range(2)]
        NQB = 3
        QT = [carve(YTLOW, i * 512, 512, BF16).rearrange("p (j t) -> p j t", j=2) for i in range(NQB)]
        GZ = [carve(YTLOW, 1536 + i * 1024, 1024).rearrange("p (j t) -> p j t", j=2) for i in range(NQB)]
        NPT = 6
        PT = [carve(YTLOW, 4608 + i * 256, 256, BF16) for i in range(NPT)]
        REC = [carve(YTLOW, 6144 + i * 256, 256) for i in range(2)]
        TT = [carve(YTLOW, 6656 + i * 256, 256) for i in range(2)]
        CSH = [carve(YTLOW, 7168 + i * 128, 128) for i in range(2)]
        SG = [carve(YTLOW, 7424, 512), carve(YTLOW, 7424, 512)]

        rKT = Res("kt")
        rVA = Res("va")
        rMSK = Res("msk")
        rCSB = [Res("csb0"), Res("csb1")]
        rQSB = [Res("qsb0"), Res("qsb1")]
        rT1 = [Res("t1a"), Res("t1b")]
        rT2 = [Res("t2a"), Res("t2b")]
        rES2 = Res("es2")
        rQT = [Res(f"qt{i}") for i in range(NQB)]
        rGZ = [Res(f"gz{i}") for i in range(NQB)]
        rPT = [Res(f"pt{i}") for i in range(NPT)]
        rSG = [Res("sg0")] * 2
        rREC = [Res("rec0"), Res("rec1")]
        rTT = [Res("tt0"), Res("tt1")]
        rCSH = Res("csh")
        rPSW = Res("pswap")

        dma("sp", MSK, masks_in[:, :].rearrange("p (w c) -> p w c", w=2), wr=[rMSK])
        dma("sp", pswap, pswap_in[:, :], wr=[rPSW])
        dma("sp", identb, ident_in[:, :], ac=[rPSW])
        rSEL = Res("sel")
        X("dve", "memset", SEL[0][0:1, 0:64], 0.0, wr=[rSEL])
        X("dve", "memset", SEL[0][0:1, 64:128], 1.0, ac=[rSEL])
        X("dve", "memset", SEL[1][0:1, 0:64], 1.0, ac=[rSEL])
        X("dve", "memset", SEL[1][0:1, 64:128], 0.0, ac=[rSEL])
        dma("sp", CSH[0], cos_in[:, 0:128], wr=[rCSH])
        dma("sp", CSH[1], sin_in[:, 0:128], ac=[rCSH])

        accb = [0]
        ropei = [0]
        SWAPB = 2
        PVB = 5

        def next_acc():
            b = accb[0] % 2
            accb[0] += 1
            return b

        def rope_unit(ab, n, cos_ap, sin_ap, rtab, dst_ap, dst_accum, swapbanks=(2, 2)):
            i = ropei[0] % 2
            ropei[0] += 1
            SWAPB = swapbanks[i]
            src = bank(ab)[:, 0:n]
            X("act", "activation", out=QSB[i][:, 0:n], in_=src, func=AF.Copy, rd=[rBANK[ab]], wr=[rQSB[i]])
            sw = bank(SWAPB)[:, 0:n]
            MM(sw, pswap, QSB[i][:, 0:n], True, True, [rQSB[i], rPSW], rBANK[SWAPB], True)
            X("dve", "tensor_tensor", out=T1[i][:, 0:n], in0=src, in1=cos_ap, op=ALU.mult,
              rd=[rBANK[ab]] + rtab, wr=[rT1[i]])
            X("dve", "tensor_tensor", out=T2[i][:, 0:n], in0=sw, in1=sin_ap, op=ALU.mult,
              rd=[rBANK[SWAPB]] + rtab, wr=[rT2[i]])
            X("dve", "tensor_tensor", out=dst_ap, in0=T1[i][:, 0:n], in1=T2[i][:, 0:n], op=ALU.add,
              rd=[rT1[i], rT2[i]], ac=dst_accum)

        def fm_matmuls(ab, s, c0, rhs_fn, rhs_res, kts=range(16)):
            for kt in kts:
                rhs = rhs_fn(kt)
                n = rhs.shape[-1]
                MM(bank(ab)[:, 0:n], WB4[:, s, kt, c0:c0 + 128], rhs, kt == 0, kt == 15,
                   [rWB[s], rhs_res[kt]], rBANK[ab], kt == 0)

        gem_next = [16]

        def gem_bg(n, rows=(6, 6), col=7):
            for i in range(n):
                if gem_next[0] < 40:
                    gemv_chunk(gem_next[0], rows[i % 2], col)
                    gem_next[0] += 1
                    if i % 2 == 1:
                        gem_flush()
            gem_flush()

        csbuf = [0]

        def load_cs(tg):
            cb = csbuf[0] % 2
            csbuf[0] += 1
            dma("sp", CSB[cb][0], cos_in[:, 128 + tg * 512:128 + (tg + 1) * 512], wr=[rCSB[cb]])
            dma("sp", CSB[cb][1], sin_in[:, 128 + tg * 512:128 + (tg + 1) * 512], ac=[rCSB[cb]])
            return cb

        LA = 4
        sgi = [0]
        for t in range(2):
            skv = load_chunk(w_in, (5 * t) * 256)
            X("dve", "memset", VAF, 1.0, wr=[rVA])
            ab = next_acc()
            fm_matmuls(ab, skv, 0, lambda kt: HTH3[:, kt, :], [rHTH] * 16)
            rope_unit(ab, 128, CSH[0], CSH[1], [rCSH], KT[:, 0:128], [rKT], (2, 3))
            for tg in range(4):
                if t == 0:
                    affine_tg(tg)
                cb = load_cs(tg)
                ab = next_acc()
                fm_matmuls(ab, skv, 0, lambda kt: HT3[:, kt, tg * 512:(tg + 1) * 512], rHT)
                rope_unit(ab, 512, CSB[cb][0], CSB[cb][1], [rCSB[cb]], KT[:, 128 + tg * 512:128 + (tg + 1) * 512], [rKT], (2, 3))
            for blk in range(17):
                ab = next_acc()
                for kt in range(16):
                    lhsT = HTH3[:, kt, :] if blk == 0 else HT3[:, kt, (blk - 1) * 128:blk * 128]
                    MM(bank(ab)[:, 0:128], lhsT, WB4[:, skv, kt, 128:256], kt == 0, kt == 15,
                       [rWB[skv], rHTH if blk == 0 else rHT[kt]], rBANK[ab], kt == 0)
                X("act", "activation", out=VA[:, blk, 0, 0:64], in_=bank(ab)[:, 0:64], func=AF.Copy, rd=[rBANK[ab]], ac=[rVA])
                X("dve", "tensor_copy", out=VA[:, blk, 1, 64:128], in_=bank(ab)[:, 64:128], rd=[rBANK[ab]], ac=[rVA])
            gem_bg(1, rows=(3, 4), col=5)
            for gp in range(2):
                sq = load_chunk(w_in, (5 * t + 1 + 2 * gp) * 256)
                sz = load_chunk(w_in, (5 * t + 2 + 2 * gp) * 256)
                for e_ in range(2):
                    for j in range(2):
                        h = 8 * t + 4 * e_ + 2 * gp + j
                        fst = (e_ == 0 and j == 0)
                        X("dve", "tensor_scalar", out=ESR[e_][0:1, j * 128:(j + 1) * 128], in0=onesf[0:1, :],
                          scalar1=esk[0:1, h:h + 1], scalar2=None, op0=ALU.mult,
                          rd=[rESK, rC], wr=[rES2] if fst else (), ac=() if fst else [rES2])

                def make_group(kind, j, tg, st):
                    qb = tg % NQB
                    box = {}
                    sw = sq if kind == "q" else sz

                    def h0():
                        if "cb" not in st:
                            st["cb"] = load_cs(tg)
                        box["ab"] = next_acc()
                        fm_matmuls(box["ab"], sw, j * 128, lambda kt: HT3[:, kt, tg * 512:(tg + 1) * 512], rHT, range(0, 8))

                    def h1():
                        ab = box["ab"]
                        cb = st["cb"]
                        fm_matmuls(ab, sw, j * 128, lambda kt: HT3[:, kt, tg * 512:(tg + 1) * 512], rHT, range(8, 16))
                        if kind == "q":
                            rope_unit(ab, 512, CSB[cb][0], CSB[cb][1], [rCSB[cb]], QT[qb][:, j, :], [rQT[qb]])
                        else:
                            si = sgi[0] % 2
                            sgi[0] += 1
                            X("act", "activation", out=SG[si], in_=bank(ab)[:, :], func=AF.Exp, scale=-1.0,
                              rd=[rBANK[ab]], wr=[rSG[si]])
                            X("act", "activation", out=SG[si], in_=SG[si], func=AF.Ln, bias=1.0, rd=[rSG[si]], wr=[rSG[si]])
                            X("act", "activation", out=SG[si], in_=SG[si], func=AF.Exp, scale=-1.0, rd=[rSG[si]], wr=[rSG[si]])
                            X("dve", "tensor_tensor", out=GZ[qb][:, j, :], in0=bank(ab)[:, :], in1=SG[si], op=ALU.mult,
                              rd=[rBANK[ab], rSG[si]], ac=[rGZ[qb]])
                    return [h0, h1]

                def make_G(tg):
                    st = {}
                    return (make_group("q", 0, tg, st) + make_group("z", 0, tg, st) +
                            make_group("q", 1, tg, st) + make_group("z", 1, tg, st))

                def S_unit(k):
                    tg, i = divmod(k, 8)
                    nb, e_ = divmod(i, 2)
                    n = 4 * tg + nb
                    qb = tg % NQB
                    scb = 3 + k % 3
                    pt = PT[k % NPT]
                    rpt = rPT[k % NPT]
                    psl = slice(e_ * 64, (e_ + 1) * 64)
                    mi = 1 if n == 0 else 0
                    MM(bank(scb)[:, :], identb, MSK[:, mi, :], True, False, [rMSK, rPSW], rBANK[scb], True)
                    for w in range(2):
                        k0 = (n + 1 - w) * 128
                        MM(bank(scb)[:, w * 256:(w + 1) * 256].rearrange("p (j q) -> p j q", j=2),
                           KT[psl, k0:k0 + 128], QT[qb][psl, :, nb * 128:(nb + 1) * 128], False, (w == 1),
                           [rKT, rQT[qb]], rBANK[scb], False)
                    X("act", "activation", out=pt, in_=bank(scb)[:, :], func=AF.Exp, scale=0.125, rd=[rBANK[scb]], wr=[rpt])

                def PV_unit(k):
                    tg, i = divmod(k, 8)
                    nb, e_ = divmod(i, 2)
                    n = 4 * tg + nb
                    qb = tg % NQB
                    pt = PT[k % NPT]
                    rpt = rPT[k % NPT]
                    u = k % 2
                    pvb = 6 + u
                    psl = slice(e_ * 64, (e_ + 1) * 64)
                    dsl = slice((1 - e_) * 64, (2 - e_) * 64)
                    MM(bank(pvb)[:, 0:256], VA[:, n, e_, :], pt[:, 256:512], True, False, [rVA, rpt], rBANK[pvb], True)
                    MM(bank(pvb)[:, 0:256], VA[:, n + 1, e_, :], pt[:, 0:256], False, True, [rVA, rpt], rBANK[pvb], False)
                    for j in range(2):
                        h = 8 * t + 4 * e_ + 2 * gp + j
                        X("act", "activation", out=REC[u][psl, j * 128:(j + 1) * 128], in_=bank(pvb)[dsl, j * 128:(j + 1) * 128],
                          func=AF.Ln, bias=esk[dsl, h:h + 1], rd=[rBANK[pvb], rESK],
                          wr=[rREC[u]] if j == 0 else (), ac=() if j == 0 else [rREC[u]])
                    X("act", "activation", out=REC[u][psl, :], in_=REC[u][psl, :], func=AF.Exp, scale=-1.0, rd=[rREC[u]], wr=[rREC[u]])
                    X("dve", "tensor_tensor", out=TT[u][psl, :], in0=bank(pvb)[psl, 0:256], in1=REC[u][psl, :],
                      op=ALU.mult, rd=[rBANK[pvb], rREC[u]], wr=[rTT[u]])
                    k0t = 8 + 4 * t + 2 * gp
                    X("dve", "tensor_tensor", out=YT3[psl, k0t:k0t + 2, n * 128:(n + 1) * 128],
                      in0=TT[u][psl, :].rearrange("p (j q) -> p j q", j=2),
                      in1=GZ[qb][psl, :, nb * 128:(nb + 1) * 128], op=ALU.mult,
                      rd=[rTT[u], rGZ[qb]], ac=[rYT[k0t], rYT[k0t + 1]])

                for f in make_G(0):
                    f()
                for tg in range(4):
                    nxt = make_G(tg + 1) if tg < 3 else []
                    for i in range(8):
                        k = 8 * tg + i
                        S_unit(k)
                        if k - LA >= 0:
                            PV_unit(k - LA)
                        for _ in range(2 if i == 0 else (1 if i < 7 else 0)):
                            if nxt:
                                nxt.pop(0)()
                for k in range(32 - LA, 32):
                    PV_unit(k)
                gem_bg(2, rows=(3, 4), col=5)
        gem_flush()
        S.barrier()
        if os.environ.get("KVERB"):
            print("ops after B", S.nadd)
        if STOP <= 2:
            S.muted = True

        VH = carve(R, 0, 8192, BF16).rearrange("p (b c) -> p b c", b=16)
        CP = carve(R, 8192, 1024).rearrange("p (g t) -> p g t", g=8)
        WSB = carve(R, 9216, 512, BF16).rearrange("p (g t) -> p g t", g=8)
        STT_ = carve(R, 9728, 192).rearrange("p (b h s) -> p b h s", b=16, h=2)
        STF = carve(R, 9728, 192)
        MV = carve(R, 9920, 32).rearrange("p (b s) -> p b s", b=16)
        RS = carve(R, 9952, 16)
        NMR = carve(R, 9968, 16)
        WSF = carve(YTLOW, 0, 1024).rearrange("p (g t) -> p g t", g=8)
        RSUM = carve(YTLOW, 1024, 1024)
        LBR = carve(YTLOW, 2048, 1024)
        BSR = carve(YTLOW, 3072, 1024)
        TRI = carve(YTLOW, 4096, 128)
        rVH = [Res(f"vh{b}") for b in range(16)]
        rCP = Res("cp")
        rWSB = Res("wsb")
        rWSF = Res("wsf")
        rST = Res("st")
        rRS = Res("rs")
        rRSUM = Res("rsum")
        rSET = Res("setA")

        dma("sp", WSF, wst_in[:, :].rearrange("p (g t) -> p g t", g=8), wr=[rWSF])
        dma("sp", TRI, tri_in[:, :], wr=[rSET])
        dma("sp", LBR[0:1, :], lb_row[:, :], ac=[rSET])
        dma("sp", BSR[0:1, :], bs_row[:, :], ac=[rSET])
        X("dve", "tensor_tensor", out=WSF, in0=WSF, in1=TRI.unsqueeze(1).broadcast_to([128, 8, 128]), op=ALU.mult,
          rd=[rSET], wr=[rWSF])
        X("dve", "tensor_copy", out=WSB, in_=WSF, rd=[rWSF], wr=[rWSB])
        for g in range(8):
            bb = 2 if g < 4 else 3
            oc = (g % 4) * 128
            MM(bank(bb)[0:1, oc:oc + 128], onesf[:, 0:1], WSF[:, g, :], True, True, [rWSF, rC], rBANK[bb], g % 4 == 0)
        X("act", "activation", out=RSUM[0:1, 0:512], in_=bank(2)[0:1, :], func=AF.Copy, rd=[rBANK[2]], wr=[rRSUM])
        X("act", "activation", out=RSUM[0:1, 512:1024], in_=bank(3)[0:1, :], func=AF.Copy, rd=[rBANK[3]], ac=[rRSUM])
        for g in range(8):
            bb = 2 if g < 4 else 3
            oc = (g % 4) * 128
            MM(bank(bb)[:, oc:oc + 128], LBR[0:1, g * 128:(g + 1) * 128], RSUM[0:1, g * 128:(g + 1) * 128], True, False,
               [rRSUM, rSET], rBANK[bb], g % 4 == 0)
            MM(bank(bb)[:, oc:oc + 128], onesf[0:1, :], BSR[0:1, g * 128:(g + 1) * 128], False, True,
               [rSET, rC], rBANK[bb], False)
        X("act", "activation", out=CP[:, 0:4, :], in_=bank(2)[:, :].rearrange("p (g t) -> p g t", g=4), func=AF.Identity,
          scale=0.5, rd=[rBANK[2]], wr=[rCP])
        X("act", "activation", out=CP[:, 4:8, :], in_=bank(3)[:, :].rearrange("p (g t) -> p g t", g=4), func=AF.Identity,
          scale=0.5, rd=[rBANK[3]], ac=[rCP])

        X("dve", "memset", STF, 0.0, wr=[rST])
        vslots = [load_chunk(w_in, (10 + i) * 256) for i in range(4)]
        for hh in range(2):
            s0, s1 = vslots[2 * hh], vslots[2 * hh + 1]
            for blk in range(16):
                ab = next_acc()
                assert s1 == s0 + 1
                for kt in range(16):
                    MM(bank(ab)[:, :].rearrange("p (a b) -> p a b", a=2), HT3[:, kt, blk * 128:(blk + 1) * 128],
                       WB4[:, s0:s0 + 2, kt, :], kt == 0, kt == 15, [rWB[s0], rWB[s1], rHT[kt]], rBANK[ab], kt == 0)
                X("dve", "bn_stats", out=STT_[:, blk, hh, :], in_=bank(ab)[:, :], rd=[rBANK[ab]], ac=[rST])
                X("act", "activation", out=VH[:, blk, hh * 512:(hh + 1) * 512], in_=bank(ab)[:, :], func=AF.Copy,
                  rd=[rBANK[ab]], ac=[rVH[blk]])
        gem_flush()
        rMV = Res("mv")
        for blk in range(16):
            X("dve", "bn_aggr", out=MV[:, blk, :], in_=STT_[:, blk, :, :].rearrange("p h s -> p (h s)"),
              rd=[rST], wr=[rMV] if blk == 0 else (), ac=() if blk == 0 else [rMV])
        X("dve", "tensor_scalar", out=RS, in0=MV[:, :, 1], scalar1=EPS, scalar2=None, op0=ALU.add, rd=[rMV], wr=[rRS])
        X("act", "activation", out=RS, in_=RS, func=AF.Sqrt, rd=[rRS], wr=[rRS])
        X("dve", "reciprocal", out=RS, in_=RS, rd=[rRS], wr=[rRS])
        X("dve", "scalar_tensor_tensor", out=NMR, in0=MV[:, :, 0], scalar=-1.0, in1=RS, op0=ALU.mult, op1=ALU.mult,
          rd=[rRS, rMV], wr=[rRS])
        for blk in range(16):
            X("dve", "tensor_scalar", out=VH[:, blk, :], in0=VH[:, blk, :], scalar1=RS[:, blk:blk + 1],
              scalar2=NMR[:, blk:blk + 1], op0=ALU.mult, op1=ALU.add, rd=[rRS], wr=[rVH[blk]])
        if STOP <= 3:
            S.muted = True

        S1 = carve(HTHW, 0, 512)
        TH = carve(HTHW, 512, 512)
        rS1 = Res("s1")
        rTH = Res("th")
        unit = [0]
        for g in range(8):
            sg = load_chunk(w_in, (14 + g) * 256)
            if g % 2 == 0:
                qd = g // 2
                dma("pool", wo_bf[qd * 512:(qd + 1) * 512, :], w_out[qd * 512:(qd + 1) * 512, :],
                    wr=[rWOBF] if qd == 0 else (), ac=() if qd == 0 else [rWOBF])
            for tg in range(4):
                b0 = 3 * (unit[0] % 2)
                unit[0] += 1
                bU, bZ, bA = b0, b0 + 1, b0 + 2
                fm_matmuls(bU, sg, 0, lambda kt: HT3[:, kt, tg * 512:(tg + 1) * 512], rHT)
                fm_matmuls(bZ, sg, 128, lambda kt: HT3[:, kt, tg * 512:(tg + 1) * 512], rHT)
                for c in range(4):
                    blk = 4 * tg + c
                    MM(bank(bA)[:, c * 128:(c + 1) * 128], VH[:, blk, g * 128:(g + 1) * 128], WSB[:, g, :], True, True,
                       [rVH[blk], rWSB], rBANK[bA], c == 0)
                X("dve", "scalar_tensor_tensor", out=S1.rearrange("p (c t) -> p c t", c=4),
                  in0=bank(bA)[:, :].rearrange("p (c t) -> p c t", c=4), scalar=lgcol[:, g:g + 1],
                  in1=CP[:, g, :].unsqueeze(1).broadcast_to([128, 4, 128]), op0=ALU.mult, op1=ALU.add,
                  rd=[rBANK[bA], rCP, rLG], wr=[rS1])
                X("dve", "tensor_tensor", out=S1, in0=S1, in1=bank(bU)[:, :], op=ALU.mult, rd=[rBANK[bU]], wr=[rS1])
                X("act", "activation", out=TH, in_=bank(bZ)[:, :], func=AF.Tanh, scale=0.5, rd=[rBANK[bZ]], wr=[rTH])
                X("dve", "scalar_tensor_tensor", out=TH, in0=TH, scalar=1.0, in1=bank(bZ)[:, :], op0=ALU.add, op1=ALU.mult,
                  rd=[rBANK[bZ]], wr=[rTH])
                X("dve", "tensor_tensor", out=YT3[:, g, tg * 512:(tg + 1) * 512], in0=S1, in1=TH, op=ALU.mult,
                  rd=[rS1, rTH], ac=[rYT[g]])
                gem_flush()
            gem_bg(2 if g < 6 else 1)
        gem_flush()
        S.barrier()

        if STOP <= 4:
            S.muted = True
        if DBG:
            dma("sp", dbg_yt[:, :], YT[:, :], rd=rYT)
            dma("sp", dbg_ht[:, :], HT[:, :], rd=rHT)
            S.barrier()
        X("dve", "tensor_tensor", out=modcol[:, 32:80], in0=modcol[:, 32:80], in1=bcol[:, 32:80], op=ALU.add,
          rd=[rMOD, rC], wr=[rMODB])
        X("dve", "scalar_tensor_tensor", out=afcol[:, :], in0=modcol[:, 64:80], scalar=1.0, in1=gfcol[:, :],
          op0=ALU.add, op1=ALU.mult, rd=[rMODB, rC], wr=[rMODB])
        XR = [carve(R, i * 2048, 2048) for i in range(2)] + [carve(WBW, 6144, 2048)]
        OUTB = [carve(R, 4096 + i * 2048, 2048) for i in range(2)]
        identf = carve(R, 8192, 128)
        TMP = [carve(R, 8320 + i * 512, 512) for i in range(2)]
        GB = carve(WBW, 0, 2048)
        AFBT = carve(WBW, 2048, 2048)
        SFB = carve(WBW, 4096, 2048)
        DG = [carve(HTHW, i * 128, 128) for i in range(2)]
        WO3 = HT3
        rXR = [Res("xr0"), Res("xr1"), Res("xr2")]
        rOUTB = [Res("ob0"), Res("ob1")]
        rTMP = [Res("tmp0"), Res("tmp1")]
        rGB = Res("gb")
        rAFB = Res("afb")
        rSFB = Res("sfb")
        rDG = [Res("dg0"), Res("dg1")]
        rSS2 = Res("ss2")
        rIDF = Res("identf")
        dma("sp", identf, identf_in[:, :], wr=[rIDF])
        for kt in range(16):
            dma("sp", WO3[:, kt, :], wo_bf[kt * 128:(kt + 1) * 128, :], rd=[rWOBF], wr=[rHT[kt]])

        def bcast_build(colap, dst, rdst):
            for c in range(16):
                i = c % 2
                X("dve", "tensor_scalar", out=DG[i], in0=identf, scalar1=colap[:, c:c + 1], scalar2=None, op0=ALU.mult,
                  rd=[rMODB, rIDF], wr=[rDG[i]])
                bb = (c // 4) % 2
                MM(bank(bb)[:, (c % 4) * 128:(c % 4 + 1) * 128], onesf[:, :], DG[i], True, True, [rDG[i], rC], rBANK[bb], c % 4 == 0)
                if c % 4 == 3:
                    q4 = c // 4
                    X("act", "activation", out=dst[:, q4 * 512:(q4 + 1) * 512], in_=bank(bb)[:, :], func=AF.Copy,
                      rd=[rBANK[bb]], wr=[rdst] if q4 == 0 else (), ac=() if q4 == 0 else [rdst])

        bcast_build(modcol[:, 32:48], GB, rGB)
        bcast_build(afcol, AFBT, rAFB)
        bcast_build(modcol[:, 48:64], SFB, rSFB)
        X("dve", "memset", sm[:, 32:96], 0.0, wr=[rSS2])
        ti = [0]
        pending_final = []
        for blk in range(16):
            i = blk % 3
            o = blk % 2
            dma("sp", XR[i], xs[(blk + 1) * 128:(blk + 2) * 128, :], wr=[rXR[i]])
            b0 = 4 * (blk % 2)
            for cg in range(4):
                for kt in range(16):
                    MM(bank(b0 + cg)[:, :], YT3[:, kt, blk * 128:(blk + 1) * 128], WO3[:, kt, cg * 512:(cg + 1) * 512],
                       kt == 0, kt == 15, [rYT[kt], rHT[kt]], rBANK[b0 + cg], kt == 0)
                tb = ti[0] % 2
                ti[0] += 1
                X("dve", "tensor_tensor", out=TMP[tb], in0=bank(b0 + cg)[:, :], in1=GB[:, cg * 512:(cg + 1) * 512], op=ALU.mult,
                  rd=[rBANK[b0 + cg], rGB], wr=[rTMP[tb]])
                X("dve", "tensor_tensor", out=XR[i][:, cg * 512:(cg + 1) * 512], in0=TMP[tb],
                  in1=XR[i][:, cg * 512:(cg + 1) * 512], op=ALU.add, rd=[rTMP[tb], rXR[i]], ac=[rXR[i]])
            ssb = sm[:, 32 + blk:33 + blk]
            rsb = sm[:, 49 + blk:50 + blk]
            if pending_final:
                pending_final.pop(0)()
            X("act", "activation", out=OUTB[o], in_=XR[i], func=AF.Square, accum_out=ssb,
              rd=[rXR[i], rSS2], wr=[rOUTB[o]], ac=[rSS2])

            def final_ops(blk=blk, i=i, o=o, ssb=ssb, rsb=rsb):
                X("dve", "tensor_scalar", out=rsb, in0=ssb, scalar1=1.0 / D, scalar2=EPS, op0=ALU.mult, op1=ALU.add,
                  rd=[rSS2], ac=[rSS2])
                X("act", "activation", out=rsb, in_=rsb, func=AF.Sqrt, rd=[rSS2], ac=[rSS2])
                X("dve", "reciprocal", out=rsb, in_=rsb, rd=[rSS2], ac=[rSS2])
                X("dve", "scalar_tensor_tensor", out=OUTB[o], in0=XR[i], scalar=rsb, in1=AFBT, op0=ALU.mult, op1=ALU.mult,
                  rd=[rXR[i], rSS2, rAFB], wr=[rOUTB[o]])
                X("dve", "tensor_tensor", out=OUTB[o], in0=OUTB[o], in1=SFB, op=ALU.add, rd=[rSFB], wr=[rOUTB[o]])
                dma("pool", out[blk * 128:(blk + 1) * 128, :], OUTB[o], rd=[rOUTB[o]])
            pending_final.append(final_ops)
        while pending_final:
            pending_final.pop(0)()
        if DBG:
            dma("sp", dbg_mod[:, :], modcol[:, :], rd=[rMODB])
        S.muted = False
        S.force = True
        S.barrier()
        S.add("sp", lambda e: e.nop())

        if os.environ.get("KTRACE"):
            lo, hi = [int(v) for v in os.environ["KTRACE"].split(",")]
            for (i, e, l) in S.trace_lbl:
                if lo <= i <= hi:
                    print(i, e, l)
        S.finalize()
        block = es.enter_context(nc.Block())

        @block.tensor
        def _(e):
            S.emit("pe", e, esem, dsems)

        @block.scalar
        def _(e):
            S.emit("act", e, esem, dsems)

        @block.vector
        def _(e):
            S.emit("dve", e, esem, dsems)

        @block.gpsimd
        def _(e):
            S.emit("pool", e, esem, dsems)

        @block.sync
        def _(e):
            S.emit("sp", e, esem, dsems)
    return nc


def _col(v, n):
    return np.ascontiguousarray(np.asarray(v, np.float32).reshape(n, 128).T)


def _prep_shared(w_ada, b_ada, norm_g, w_in, ln_v_g, ln_v_b, w_spatial, b_spatial, sinks, w_out,
                 w_ada_final, b_ada_final, final_norm_g):
    f32 = np.float32
    w_in0 = np.asarray(w_in[0], f32)
    cols = []
    for t in range(2):
        kv = []
        for m in (2 * t, 2 * t + 1):
            kv.append(np.arange(4096 + m * 64, 4096 + (m + 1) * 64))
        for m in (2 * t, 2 * t + 1):
            kv.append(np.arange(4352 + m * 64, 4352 + (m + 1) * 64))
        cols.append(np.concatenate(kv))
        for gp in range(2):
            for base in (3072, 4608):
                cc = []
                for j in range(2):
                    g = 2 * gp + j
                    for h in (8 * t + g, 8 * t + 4 + g):
                        cc.append(np.arange(base + h * 64, base + (h + 1) * 64))
                cols.append(np.concatenate(cc))
    cols.append(np.arange(1024, 2048))
    for g in range(8):
        cols.append(np.arange(g * 128, (g + 1) * 128))
        cols.append(np.arange(2048 + g * 128, 2048 + (g + 1) * 128))
    perm = np.concatenate(cols)
    assert perm.shape[0] == 5632 and np.unique(perm).shape[0] == 5632
    w_in_p = np.ascontiguousarray(w_in0[:, perm])
    w_out0 = np.asarray(w_out[0], f32)
    rows = [np.arange(0, 1024)]
    for t in range(2):
        for g in range(4):
            for h in (8 * t + g, 8 * t + 4 + g):
                rows.append(np.arange(1024 + h * 64, 1024 + (h + 1) * 64))
    rperm = np.concatenate(rows)
    assert rperm.shape[0] == 2048 and np.unique(rperm).shape[0] == 2048
    w_out_p = np.ascontiguousarray(w_out0[rperm, :])
    b_col = np.concatenate([_col(b_ada[0], 48), _col(b_ada_final, 32)], axis=1)
    wst = np.ascontiguousarray(np.transpose(np.asarray(w_spatial[0], f32), (2, 0, 1)).reshape(128, 8 * 128))
    tri = (np.arange(128)[:, None] <= np.arange(128)[None, :]).astype(f32)
    ident = np.eye(128, dtype=f32)
    pswap = np.zeros((128, 128), f32)
    for base in (0, 64):
        for m in range(32):
            pswap[base + m + 32, base + m] = -1.0
            pswap[base + m, base + m + 32] = 1.0
    shared = {
        "w_ada": np.ascontiguousarray(np.asarray(w_ada[0], f32)),
        "w_adaf": np.ascontiguousarray(np.asarray(w_ada_final, f32)),
        "b_col": np.ascontiguousarray(b_col),
        "g_col": _col(norm_g[0], 16),
        "gf_col": _col(final_norm_g, 16),
        "w_in": w_in_p,
        "w_out": w_out_p,
        "lg_col": _col(ln_v_g[0], 8),
        "lb_row": np.ascontiguousarray(np.asarray(ln_v_b[0], f32).reshape(1, 1024)),
        "wst": wst,
        "bs_row": np.ascontiguousarray(np.asarray(b_spatial[0], f32).reshape(1, 1024)),
        "tri": tri,
        "sinks_b": np.ascontiguousarray(np.broadcast_to(np.asarray(sinks[0], f32).reshape(1, 16), (128, 16))),
        "ident": ident.astype(ml_dtypes.bfloat16),
        "identf": ident,
        "pswap": pswap.astype(ml_dtypes.bfloat16),
    }
    return shared


def _rope_tables(half):
    f32 = np.float32
    pos_i = np.arange(17 * 128, dtype=np.int64) + half * 2048 - 128
    cos = sin = None
    try:
        import jax
        import jax.numpy as jnp
        cpu = jax.devices("cpu")[0]
        with jax.default_device(cpu):
            inv_freq_j = 10000.0 ** (-jnp.arange(0, 64, 2, dtype=jnp.float32) / 64)
            ang_j = jnp.asarray(pos_i.astype(f32))[:, None] * inv_freq_j[None, :]
            cos = np.asarray(jnp.cos(ang_j), dtype=f32).T
            sin = np.asarray(jnp.sin(ang_j), dtype=f32).T
    except Exception:
        cos = sin = None
    if cos is None:
        inv_freq = (np.float32(10000.0) ** (-(np.arange(0, 64, 2, dtype=f32)) / np.float32(64))).astype(f32)
        ang = (pos_i.astype(f32)[:, None] * inv_freq[None, :]).astype(f32)
        cos = np.cos(ang).astype(f32).T
        sin = np.sin(ang).astype(f32).T
    cosT = np.ascontiguousarray(np.tile(cos, (4, 1)))
    sinT = np.ascontiguousarray(np.tile(sin, (4, 1)))
    return cosT, sinT


def _masks(half):
    j = np.arange(128)[:, None]
    q = np.arange(128)[None, :]
    cur = (j <= q).astype(np.float32)
    prev = (j > q).astype(np.float32)
    pf = prev if half == 1 else np.zeros_like(prev)
    m0 = np.concatenate([cur, cur, prev, prev], axis=1)
    m1 = np.concatenate([cur, cur, pf, pf], axis=1)
    valid = np.concatenate([m0, m1], axis=1)
    bias = np.where(valid > 0.5, np.float32(0.0), np.float32(-30000.0)).astype(np.float32)
    return np.ascontiguousarray(bias).astype(ml_dtypes.bfloat16)


_NC_CACHE = {}


def kernel(x, c, w_ada, b_ada, norm_g, w_in, ln_v_g, ln_v_b, w_spatial, b_spatial, sinks, w_out,
           w_ada_final, b_ada_final, final_norm_g):
    x = np.asarray(x, np.float32)
    c = np.asarray(c, np.float32)
    shared = _prep_shared(w_ada, b_ada, norm_g, w_in, ln_v_g, ln_v_b, w_spatial, b_spatial, sinks, w_out,
                          w_ada_final, b_ada_final, final_norm_g)
    in_maps = []
    for core in range(NCORES):
        b, half = core // 2, core % 2
        own = x[b, half * 2048:(half + 1) * 2048]
        halo = x[b, 1920:2048] if half == 1 else np.zeros((128, D), np.float32)
        cosT, sinT = _rope_tables(half)
        m = dict(shared)
        m["xs"] = np.ascontiguousarray(np.concatenate([halo, own], axis=0))
        m["c_col"] = _col(c[b], 16)
        m["masks"] = _masks(half)
        m["cosT"] = cosT
        m["sinT"] = sinT
        in_maps.append(m)
    if "nc" not in _NC_CACHE:
        _NC_CACHE["nc"] = build_nc()
    nc = _NC_CACHE["nc"]
    res = run_bass_kernel_spmd(nc, in_maps, core_ids=list(range(NCORES)))
    outp = np.empty((4, 4096, D), np.float32)
    for core in range(NCORES):
        b, half = core // 2, core % 2
        outp[b, half * 2048:(half + 1) * 2048] = np.asarray(res.results[core]["out"], np.float32)
    return outp
```
